# Optimizing a Trainium2 kernel written in Bass

```python
import math
import jax, jax.numpy as jnp
from jax import lax
import numpy as np

D_MODEL = 2048
BATCH = 2
SEQ = 8192
DEPTH = 1

HEAD_DIM = 128
MIX_WIDTH = D_MODEL
RET_WIDTH = MIX_WIDTH // 2
ATTN_WIDTH = MIX_WIDTH - RET_WIDTH
RET_HEADS = RET_WIDTH // HEAD_DIM
ATTN_HEADS = ATTN_WIDTH // HEAD_DIM
ATTN_KV_HEADS = ATTN_HEADS // 4
KV_WIDTH = ATTN_KV_HEADS * HEAD_DIM
IN_WIDTH = 4 * RET_WIDTH + ATTN_WIDTH + 2 * KV_WIDTH
RET_CHUNK = 128
Q_BLOCK = 128
GRID_W = 64
AXIS_DIM = HEAD_DIM // 2
ROPE_THETA = 10000.0
MEM_TOKENS = 256
CROSS_HEADS = 4
CROSS_HEAD_DIM = 128
CROSS_WIDTH = CROSS_HEADS * CROSS_HEAD_DIM
D_FF = 4 * D_MODEL
NORM_EPS = 1e-6

kernel_name = "hybrid_retention_gqa_encoder_block"


def rms_norm(x, w):
    xf = x.astype(jnp.float32)
    y = xf * lax.rsqrt(jnp.mean(xf * xf, axis=-1, keepdims=True) + NORM_EPS)
    return (y * w.astype(jnp.float32)).astype(x.dtype)


def axial_rope_tables(seq_len):
    rows = seq_len // GRID_W
    row = jnp.repeat(jnp.arange(rows, dtype=jnp.float32), GRID_W)
    col = jnp.tile(jnp.arange(GRID_W, dtype=jnp.float32), rows)
    inv_freq = 1.0 / (ROPE_THETA ** (jnp.arange(0, AXIS_DIM, 2, dtype=jnp.float32) / AXIS_DIM))
    ang_r = row[:, None] * inv_freq[None, :]
    ang_c = col[:, None] * inv_freq[None, :]
    return (jnp.cos(ang_r), jnp.sin(ang_r), jnp.cos(ang_c), jnp.sin(ang_c))


def _rope_half(x, cos, sin):
    x1, x2 = jnp.split(x, 2, axis=-1)
    cos = cos.astype(x.dtype)
    sin = sin.astype(x.dtype)
    return jnp.concatenate([x1 * cos - x2 * sin, x2 * cos + x1 * sin], axis=-1)


def apply_axial_rope(x, rope):
    cos_r, sin_r, cos_c, sin_c = rope
    xr, xc = jnp.split(x, 2, axis=-1)
    return jnp.concatenate([_rope_half(xr, cos_r, sin_r), _rope_half(xc, cos_c, sin_c)], axis=-1)


def retention_one_direction(q, k, v, log_gamma, strict):
    B, H, S, dk = q.shape
    dv = v.shape[-1]
    n_chunks = S // RET_CHUNK

    def to_chunks(t):
        return t.reshape(B, H, n_chunks, RET_CHUNK, t.shape[-1]).transpose(2, 0, 1, 3, 4)

    qc, kc, vc = to_chunks(q), to_chunks(k), to_chunks(v)
    idx = jnp.arange(RET_CHUNK, dtype=jnp.float32)
    rel = idx[:, None] - idx[None, :]
    mask = rel > 0 if strict else rel >= 0
    lg = log_gamma[:, None, None]
    decay_inner = jnp.where(mask[None], jnp.exp(lg * jnp.where(mask, rel, 0.0)[None]), 0.0)
    decay_query = jnp.exp(log_gamma[:, None] * (idx + 1.0)[None, :])[..., None]
    decay_key = jnp.exp(log_gamma[:, None] * (RET_CHUNK - 1.0 - idx)[None, :])[..., None]
    decay_chunk = jnp.exp(log_gamma * RET_CHUNK)[:, None, None]

    def step(state, inp):
        q_i, k_i, v_i = inp
        scores = jnp.einsum('bhnd,bhmd->bhnm', q_i, k_i) * decay_inner
        out = jnp.einsum('bhnm,bhmv->bhnv', scores, v_i)
        out = out + jnp.einsum('bhnd,bhdv->bhnv', q_i, state) * decay_query
        state = state * decay_chunk + jnp.einsum('bhmd,bhmv->bhdv', k_i * decay_key, v_i)
        return state, out

    state0 = jnp.zeros((B, H, dk, dv), jnp.float32)
    _, out = lax.scan(step, state0, (qc, kc, vc))
    return out.transpose(1, 2, 0, 3, 4).reshape(B, H, S, dv)


def bidirectional_retention(q, k, v, log_gamma_fwd, log_gamma_bwd):
    fwd = retention_one_direction(q, k, v, log_gamma_fwd, strict=False)
    flip = lambda t: jnp.flip(t, axis=2)
    bwd = flip(retention_one_direction(flip(q), flip(k), flip(v), log_gamma_bwd, strict=True))
    return fwd + bwd


def block_gqa_attention(q, k, v):
    B, H, S, d = q.shape
    kvh = k.shape[1]
    groups = H // kvh
    n_blocks = S // Q_BLOCK
    scale = d ** -0.5
    qb = q.reshape(B, kvh, groups, n_blocks, Q_BLOCK, d).transpose(3, 0, 1, 2, 4, 5)

    def one_block(q_blk):
        s = jnp.einsum('bkgqd,bksd->bkgqs', q_blk, k).astype(jnp.float32) * scale
        p = jax.nn.softmax(s, axis=-1)
        return jnp.einsum('bkgqs,bksd->bkgqd', p.astype(v.dtype), v)

    o = lax.map(one_block, qb)
    return o.transpose(1, 0, 4, 2, 3, 5).reshape(B, S, H * d)


def hybrid_mixer(h, w_in, ret_decay_fwd, ret_decay_bwd, ret_gn_w, ret_gn_b,
                 attn_q_norm_w, attn_k_norm_w, w_out, rope):
    B, S, _ = h.shape
    proj = h @ w_in
    c1 = RET_WIDTH
    c2 = 2 * RET_WIDTH
    c3 = 3 * RET_WIDTH
    c4 = 4 * RET_WIDTH
    c5 = c4 + ATTN_WIDTH
    c6 = c5 + KV_WIDTH
    rq, rk, rv, rg, aq, ak, av = jnp.split(proj, [c1, c2, c3, c4, c5, c6], axis=-1)

    def heads(t, n):
        return t.reshape(B, S, n, HEAD_DIM).transpose(0, 2, 1, 3)

    rq = apply_axial_rope(heads(rq, RET_HEADS), rope).astype(jnp.float32)
    rk = (apply_axial_rope(heads(rk, RET_HEADS), rope).astype(jnp.float32)) * (HEAD_DIM ** -0.5)
    rv = heads(rv, RET_HEADS).astype(jnp.float32)
    log_g_f = -jnp.exp(ret_decay_fwd.astype(jnp.float32))
    log_g_b = -jnp.exp(ret_decay_bwd.astype(jnp.float32))
    y = bidirectional_retention(rq, rk, rv, log_g_f, log_g_b)
    mu = jnp.mean(y, axis=-1, keepdims=True)
    var = jnp.mean(jnp.square(y - mu), axis=-1, keepdims=True)
    y = ((y - mu) * lax.rsqrt(var + NORM_EPS)).transpose(0, 2, 1, 3).reshape(B, S, RET_WIDTH)
    y = y * ret_gn_w.astype(jnp.float32) + ret_gn_b.astype(jnp.float32)
    y_ret = (jax.nn.silu(rg.astype(jnp.float32)) * y).astype(h.dtype)

    aq = apply_axial_rope(rms_norm(heads(aq, ATTN_HEADS), attn_q_norm_w), rope)
    ak = apply_axial_rope(rms_norm(heads(ak, ATTN_KV_HEADS), attn_k_norm_w), rope)
    av = heads(av, ATTN_KV_HEADS)
    y_attn = block_gqa_attention(aq, ak, av)

    return jnp.concatenate([y_ret, y_attn], axis=-1) @ w_out


def memory_cross_attention(h, m, wq, wk, wv, wo):
    B, S, _ = h.shape
    M = m.shape[1]
    q = (h @ wq).reshape(B, S, CROSS_HEADS, CROSS_HEAD_DIM)
    k = (m @ wk).reshape(B, M, CROSS_HEADS, CROSS_HEAD_DIM)
    v = (m @ wv).reshape(B, M, CROSS_HEADS, CROSS_HEAD_DIM)
    s = jnp.einsum('bshd,bmhd->bhsm', q, k).astype(jnp.float32) * (CROSS_HEAD_DIM ** -0.5)
    p = jax.nn.softmax(s, axis=-1)
    o = jnp.einsum('bhsm,bmhd->bshd', p.astype(v.dtype), v).reshape(B, S, CROSS_WIDTH)
    return o @ wo


def setup_inputs(seed: int = 0) -> dict:
    key = jax.random.key(seed)
    ks = jax.random.split(key, 24)

    def w(k, shape, fan_in):
        return jax.random.normal(k, shape, jnp.float32) * (fan_in ** -0.5)

    def gain(k, shape):
        return 1.0 + 0.02 * jax.random.normal(k, shape, jnp.float32)

    base = jnp.log(-jnp.log1p(-(2.0 ** (-(5.0 + jnp.arange(RET_HEADS, dtype=jnp.float32))))))
    return {
        "x": jax.random.normal(ks[0], (BATCH, SEQ, D_MODEL), jnp.float32),
        "mem": jax.random.normal(ks[1], (BATCH, MEM_TOKENS, D_MODEL), jnp.float32),
        "norm_mix_w": gain(ks[2], (DEPTH, D_MODEL)),
        "w_in": w(ks[3], (DEPTH, D_MODEL, IN_WIDTH), D_MODEL),
        "ret_decay_fwd": base[None, :] + 0.05 * jax.random.normal(ks[4], (DEPTH, RET_HEADS), jnp.float32),
        "ret_decay_bwd": base[None, :] + 0.05 * jax.random.normal(ks[5], (DEPTH, RET_HEADS), jnp.float32),
        "ret_gn_w": gain(ks[6], (DEPTH, RET_WIDTH)),
        "ret_gn_b": 0.02 * jax.random.normal(ks[7], (DEPTH, RET_WIDTH), jnp.float32),
        "attn_q_norm_w": gain(ks[8], (DEPTH, HEAD_DIM)),
        "attn_k_norm_w": gain(ks[9], (DEPTH, HEAD_DIM)),
        "w_out": w(ks[10], (DEPTH, MIX_WIDTH, D_MODEL), MIX_WIDTH),
        "norm_cross_w": gain(ks[11], (DEPTH, D_MODEL)),
        "norm_mem_w": gain(ks[12], (DEPTH, D_MODEL)),
        "w_cross_q": w(ks[13], (DEPTH, D_MODEL, CROSS_WIDTH), D_MODEL),
        "w_cross_k": w(ks[14], (DEPTH, D_MODEL, CROSS_WIDTH), D_MODEL),
        "w_cross_v": w(ks[15], (DEPTH, D_MODEL, CROSS_WIDTH), D_MODEL),
        "w_cross_o": w(ks[16], (DEPTH, CROSS_WIDTH, D_MODEL), CROSS_WIDTH),
        "norm_mlp_w": gain(ks[17], (DEPTH, D_MODEL)),
        "w_mlp_up": w(ks[18], (DEPTH, D_MODEL, D_FF), D_MODEL),
        "w_mlp_down": w(ks[19], (DEPTH, D_FF, D_MODEL), D_FF),
        "norm_final_w": gain(ks[20], (D_MODEL,)),
    }


def reference(x, mem, norm_mix_w, w_in, ret_decay_fwd, ret_decay_bwd, ret_gn_w, ret_gn_b,
              attn_q_norm_w, attn_k_norm_w, w_out, norm_cross_w, norm_mem_w,
              w_cross_q, w_cross_k, w_cross_v, w_cross_o, norm_mlp_w,
              w_mlp_up, w_mlp_down, norm_final_w):
    rope = axial_rope_tables(x.shape[1])
    for l in range(DEPTH):
        h = rms_norm(x, norm_mix_w[l])
        x = x + hybrid_mixer(h, w_in[l], ret_decay_fwd[l], ret_decay_bwd[l], ret_gn_w[l], ret_gn_b[l],
                             attn_q_norm_w[l], attn_k_norm_w[l], w_out[l], rope)
        h = rms_norm(x, norm_cross_w[l])
        m = rms_norm(mem, norm_mem_w[l])
        x = x + memory_cross_attention(h, m, w_cross_q[l], w_cross_k[l], w_cross_v[l], w_cross_o[l])
        h = rms_norm(x, norm_mlp_w[l])
        x = x + jnp.square(jax.nn.relu(h @ w_mlp_up[l])) @ w_mlp_down[l]
    return rms_norm(x, norm_final_w)
```

```python
import math
from contextlib import ExitStack

import numpy as np
import concourse.bass as bass
import concourse.mybir as mybir
from concourse.bass_utils import run_bass_kernel_spmd

F32 = mybir.dt.float32
BF16 = mybir.dt.bfloat16
AF = mybir.ActivationFunctionType
ALU = mybir.AluOpType
AX = mybir.AxisListType

COMPUTE = ("pe", "act", "dve", "pool")
ENGS = ("pe", "act", "dve", "pool", "sp")
EIDX = {e: i for i, e in enumerate(COMPUTE)}

D = 2048
DC = 16
DFF = 8192
NRH = 8
EPS = 1e-6
BIG = 1.0e9
LNSC = -0.5 * math.log(128.0)
ISQ = 128.0 ** -0.5


class _Op:
    __slots__ = ("fn", "waits", "inc", "dma")

    def __init__(self, fn, dma=None):
        self.fn = fn
        self.waits = []
        self.inc = False
        self.dma = dma


class Sched:
    def __init__(self, nslots=16):
        self.ops = {e: [] for e in ENGS}
        self.clock = {e: [-1] * len(COMPUTE) for e in ENGS}
        self.snap = {e: [] for e in ENGS}
        self.known_d = {e: {} for e in ENGS}
        self.last_w = {}
        self.readers = {}
        self.nslots = nslots
        self.slot_target = {}
        self.slot_rr = {q: 0 for q in ENGS}
        self.dma_pending = []
        self.last_real = {e: -1 for e in COMPUTE}

    def _need(self, eng, ev, op):
        if ev[0] == "c":
            _, x, i = ev
            xi = EIDX[x]
            if self.clock[eng][xi] >= i:
                return
            op.waits.append(ev)
            self.ops[x][i].inc = True
            sn = self.snap[x][i]
            ck = self.clock[eng]
            for k in range(len(COMPUTE)):
                if sn[k] > ck[k]:
                    ck[k] = sn[k]
            if ck[xi] < i:
                ck[xi] = i
        else:
            _, slot, target, sn = ev
            if self.known_d[eng].get(slot, 0) >= target:
                return
            op.waits.append(ev)
            self.known_d[eng][slot] = target
            if sn is not None:
                ck = self.clock[eng]
                for k in range(len(COMPUTE)):
                    if sn[k] > ck[k]:
                        ck[k] = sn[k]

    def _deps(self, eng, op, reads, writes, is_dma):
        for r in reads:
            w = self.last_w.get(r)
            if w is not None:
                if not (w[0] == "c" and w[1] == eng and not is_dma and eng == "pe"):
                    self._need(eng, w, op)
            if isinstance(r, tuple) and r[0] == "ps":
                rd = self.readers.get(r)
                if rd:
                    for x, i in list(rd[0].items()):
                        if x != eng:
                            self._need(eng, ("c", x, i), op)
        for r in writes:
            w = self.last_w.get(r)
            if w is not None:
                same = w[0] == "c" and w[1] == eng and not is_dma and eng == "pe"
                if not same:
                    self._need(eng, w, op)
            rd = self.readers.get(r)
            if rd:
                for x, i in rd[0].items():
                    if x == eng and not is_dma and eng == "pe":
                        continue
                    self._need(eng, ("c", x, i), op)
                for dev in rd[1]:
                    self._need(eng, dev, op)

    def _commit(self, ev, reads, writes):
        ws = set(writes)
        for r in writes:
            self.last_w[r] = ev
            self.readers[r] = [{}, []]
        for r in reads:
            if r in ws:
                continue
            rd = self.readers.setdefault(r, [{}, []])
            if ev[0] == "c":
                if rd[0].get(ev[1], -1) < ev[2]:
                    rd[0][ev[1]] = ev[2]
            else:
                rd[1].append(ev)

    def op(self, eng, fn, reads=(), writes=()):
        o = _Op(fn)
        idx = len(self.ops[eng])
        self._deps(eng, o, reads, writes, False)
        self.ops[eng].append(o)
        self.snap[eng].append(tuple(self.clock[eng]))
        self.last_real[eng] = idx
        self._commit(("c", eng, idx), reads, writes)
        return o

    def dma(self, q, out, in_, reads=(), writes=(), **kw):
        o = _Op(None, dma=True)
        self._deps(q, o, reads, writes, True)
        slot = (q, self.slot_rr[q] % self.nslots)
        self.slot_rr[q] += 1
        prev = self.slot_target.get(slot, 0)
        if prev and self.known_d[q].get(slot, 0) < prev:
            o.waits.append(("d", slot, prev, None))
            self.known_d[q][slot] = prev
        target = prev + 16
        self.slot_target[slot] = target
        o.dma = (slot, out, in_, kw)
        self.ops[q].append(o)
        sn = tuple(self.clock[q])
        self.snap[q].append(sn)
        ev = ("d", slot, target, sn)
        self._commit(ev, reads, writes)
        self.dma_pending.append(ev)
        return o

    def barrier(self):
        evs = [("c", x, self.last_real[x]) for x in COMPUTE if self.last_real[x] >= 0]
        devs = list(self.dma_pending)
        self.dma_pending = []
        for e in ENGS:
            o = _Op(None)
            for ev in evs:
                if ev[1] == e and e == "pe":
                    continue
                self._need(e, ev, o)
            for ev in devs:
                self._need(e, ev, o)
            self.ops[e].append(o)
            self.snap[e].append(tuple(self.clock[e]))
        self.last_w = {}
        self.readers = {}

    def finish(self):
        o = _Op(None)
        for ev in self.dma_pending:
            self._need("sp", ev, o)
        self.ops["sp"].append(o)
        self.snap["sp"].append(tuple(self.clock["sp"]))

    def emit(self, nc, stack):
        sems = {e: stack.enter_context(nc.semaphore("c_" + e)) for e in COMPUTE}
        slot_sem = {}
        for slot in self.slot_target:
            slot_sem[slot] = stack.enter_context(nc.semaphore("d_%s_%d" % slot))
        prefix = {}
        for x in COMPUTE:
            c = 0
            p = []
            for o in self.ops[x]:
                if o.inc:
                    c += 1
                p.append(c)
            prefix[x] = p
        block = stack.enter_context(nc.Block())
        ops = self.ops

        def run(engname):
            def body(e):
                for o in ops[engname]:
                    for w in o.waits:
                        if w[0] == "c":
                            e.wait_ge(sems[w[1]], prefix[w[1]][w[2]])
                        else:
                            e.wait_ge(slot_sem[w[1]], w[2])
                    if o.dma:
                        slot, out, in_, kw = o.dma
                        e.dma_start(out=out, in_=in_, **kw).then_inc(slot_sem[slot], 16)
                    elif o.fn is not None:
                        ins = o.fn(e)
                        if o.inc:
                            ins.then_inc(sems[engname], 1)
            return body

        if ops["pe"]:
            block.tensor(run("pe"))
        if ops["act"]:
            block.scalar(run("act"))
        if ops["dve"]:
            block.vector(run("dve"))
        if ops["pool"]:
            block.gpsimd(run("pool"))
        if ops["sp"]:
            block.sync(run("sp"))


SM_NW = 0
SM_DEC = 64
SM_IDX = 80
SM_GQ = 84
SM_REL = 84 + 512
SM_EO = SM_REL + 256


def build(SEQ, debug=None, stop=None):
    TOK = SEQ // 4
    NT = TOK // 128
    NO = 3 * NT
    NK = 4 * NT
    GT = 4 if NT >= 4 else NT
    NG = NT // GT
    GW = GT * 128
    NSM = SM_EO + 2 * NO

    nc = bass.Bass("TRN2", target_bir_lowering=False)

    def din(name, shape, dt=F32):
        return nc.dram_tensor(name, list(shape), dt, kind="ExternalInput").ap()

    x_own = din("x_own", [TOK, D])
    x_oth = din("x_oth", [3 * TOK, D])
    cs_own = din("cs_own", [TOK, 256])
    cs_oth = din("cs_oth", [3 * TOK, 256])
    smalls = din("smalls", [128, NSM])
    ident_d = din("ident", [128, 128])
    gnwb_d = din("gnwb", [128, 2048])
    wfin_d = din("wfin", [128, D])
    mem_d = din("mem", [256, D])
    w_in = din("w_in", [D, 5632])
    w_out = din("w_out", [D, D])
    w_cq = din("w_cq", [D, 512])
    w_ck = din("w_ck", [D, 512])
    w_cv = din("w_cv", [D, 512])
    w_co = din("w_co", [512, D])
    w_up = din("w_up", [D, DFF])
    w_dn = din("w_dn", [DFF, D])
    out_d = nc.dram_tensor("out", [TOK, D], F32, kind="ExternalOutput").ap()
    ytr = nc.dram_tensor("ytr", [128, 8, TOK], BF16).ap()
    ksc = nc.dram_tensor("ksc", [128, 2, NK * 128], BF16).ap()
    vsc = nc.dram_tensor("vsc", [128, NK, 256], BF16).ap()
    dbg_d = None
    if debug is not None:
        dbg_d = nc.dram_tensor("dbg", list(debug[1]), F32, kind="ExternalOutput").ap()

    w_in_v = w_in.rearrange("(dc p) n -> p dc n", p=128)
    w_out_v = w_out.rearrange("(dc p) n -> p dc n", p=128)
    w_cq_v = w_cq.rearrange("(dc p) n -> p dc n", p=128)
    w_ck_v = w_ck.rearrange("(dc p) n -> p dc n", p=128)
    w_cv_v = w_cv.rearrange("(dc p) n -> p dc n", p=128)
    w_co_v = w_co.rearrange("(dc p) n -> p dc n", p=128)
    w_up_v = w_up.rearrange("(dc p) n -> p dc n", p=128)
    w_dn_v = w_dn.rearrange("(f p) n -> p f n", p=128)

    S = Sched()
    stack = ExitStack()
    ARENA_BYTES = 212832
    arena = stack.enter_context(nc.sbuf_tensor("arena", [128, ARENA_BYTES // 4], F32))
    psd = [stack.enter_context(nc.psum_tensor("psd%d" % i, [128, 1024], F32)) for i in range(4)]

    class Alloc:
        def __init__(self, base):
            self.off = base

        def take(self, shape, dt):
            n = 1
            for s in shape:
                n *= s
            nb = n * (2 if dt == BF16 else 4)
            nb = (nb + 63) // 64 * 64
            off = self.off
            self.off += nb
            assert self.off <= ARENA_BYTES, ("SBUF arena overflow", self.off)
            a = arena[:, off // 4:(off + nb) // 4]
            if dt == BF16:
                a = a.bitcast(BF16)
            a = a[:, 0:n]
            if len(shape) == 2:
                a = a.rearrange("p (a b) -> p a b", a=shape[0], b=shape[1])
            elif len(shape) == 3:
                a = a.rearrange("p (a b c) -> p a b c", a=shape[0], b=shape[1], c=shape[2])
            return a

    def ps_full(b):
        return psd[b // 2][:, (b % 2) * 512:(b % 2 + 1) * 512], [("ps", b)]

    def ps_half_bf(b, h):
        o = (b % 2) * 512 + h * 256
        a = psd[b // 2][:, o:o + 256].bitcast(BF16)
        return a, [("ps", b)]

    rot = {"mm": 0, "tr": 0}

    def next_mm():
        b = 2 + rot["mm"] % 4
        rot["mm"] += 1
        return ps_full(b)

    def next_tr():
        k = rot["tr"] % 2
        rot["tr"] += 1
        return ps_half_bf(k, 0)

    P = Alloc(0)
    ident = P.take([128], BF16)
    ones = P.take([128], BF16)
    sm = P.take([NSM], F32)
    lg = P.take([16], F32)
    dcy = P.take([16], F32)
    tabs = P.take([4, 8], F32)
    cst = P.take([8], F32)
    KcT = P.take([4, 256], BF16)
    Vc = P.take([2, 512], BF16)
    sc = P.take([16, 4], F32)
    PBASE = P.off

    lnsc = cst[:, 0:1]
    epsc = cst[:, 1:2]
    zero = cst[:, 2:3]
    negc = cst[:, 3:4]
    scn = [0]

    def next_sc():
        k = scn[0] % 16
        scn[0] += 1
        return sc[:, k, :], ("sc", k)

    S.dma("sp", out=sm, in_=smalls, writes=["sm"])
    S.dma("pool", out=ident, in_=ident_d, writes=["ident"])
    S.op("dve", lambda e: e.memset(ones, 1.0), writes=["ones"])
    S.op("dve", lambda e: e.memset(cst[:, 0:1], LNSC), writes=["cst0"])
    S.op("dve", lambda e: e.memset(cst[:, 1:2], EPS), writes=["cst1"])
    S.op("dve", lambda e: e.memset(cst[:, 2:3], 0.0), writes=["cst2"])
    CST = ["cst0", "cst1", "cst2"]
    S.op("act", lambda e: e.activation(out=lg, in_=sm[:, SM_DEC:SM_DEC + 16], func=AF.Exp),
         reads=["sm"], writes=["lg"])
    S.op("dve", lambda e: e.tensor_scalar(out=lg, in0=lg, scalar1=-1.0, scalar2=None, op0=ALU.mult),
         reads=["lg"], writes=["lg"])
    S.op("act", lambda e: e.activation(out=dcy, in_=lg, func=AF.Exp, scale=128.0),
         reads=["lg"], writes=["dcy"])
    for k in range(4):
        d0 = 0 if k in (0, 2) else 8
        S.op("dve", lambda e, k=k, d0=d0: e.tensor_scalar(
            out=tabs[:, k, :], in0=lg[:, d0:d0 + 8], scalar1=sm[:, SM_IDX + k:SM_IDX + k + 1],
            scalar2=None, op0=ALU.mult), reads=["lg", "sm"], writes=[("tab", k)])
        S.op("act", lambda e, k=k: e.activation(
            out=tabs[:, k, :], in_=tabs[:, k, :], func=AF.Exp, bias=(lnsc if k >= 2 else zero)),
            reads=[("tab", k)] + CST, writes=[("tab", k)])
    DQF, DQB, WKF, WKB = (tabs[:, k, :] for k in range(4))
    TABK = [("tab", k) for k in range(4)]
    S.op("dve", lambda e: e.tensor_reduce(out=cst[:, 4:5], in_=sm[:, SM_GQ:SM_GQ + 128], axis=AX.X,
                                          op=ALU.max, apply_absolute_value=True), reads=["sm"], writes=["cst4"])
    S.op("dve", lambda e: e.tensor_reduce(out=cst[:, 5:6], in_=sm[:, SM_GQ + 256:SM_GQ + 384], axis=AX.X,
                                          op=ALU.max, apply_absolute_value=True), reads=["sm"], writes=["cst5"])
    S.op("dve", lambda e: e.tensor_tensor(out=cst[:, 3:4], in0=cst[:, 4:5], in1=cst[:, 5:6], op=ALU.mult),
         reads=["cst4", "cst5"], writes=["negc"])
    S.op("dve", lambda e: e.tensor_scalar(out=cst[:, 3:4], in0=cst[:, 3:4], scalar1=-math.sqrt(128.0),
                                          scalar2=None, op0=ALU.mult), reads=["negc"], writes=["negc"])

    def norm_a(src_dram, xt, xt_key, xn, xn_key, load=True):
        if load:
            S.dma("sp", out=xt, in_=src_dram, writes=[xt_key])
        s4, sk = next_sc()
        S.op("act", lambda e: e.activation(out=xn, in_=xt, func=AF.Square, accum_out=s4[:, 0:1]),
             reads=[xt_key], writes=[xn_key, sk])
        S.op("act", lambda e: e.activation(out=s4[:, 1:2], in_=s4[:, 0:1], func=AF.Sqrt,
                                           scale=1.0 / D, bias=epsc), reads=[sk] + CST, writes=[sk])
        S.op("dve", lambda e: e.reciprocal(out=s4[:, 2:3], in_=s4[:, 1:2]), reads=[sk], writes=[sk])
        S.op("dve", lambda e: e.tensor_scalar(out=xn, in0=xt, scalar1=s4[:, 2:3], scalar2=None,
                                              op0=ALU.mult), reads=[xt_key, sk], writes=[xn_key])

    def norm_b(xn, xn_key, nwbase, dst4, dst_key):
        for g in range(4):
            pt, pk = next_tr()
            ptv = pt.rearrange("p (a b) -> p a b", a=4, b=128)
            for j in range(4):
                c = (4 * g + j) * 128
                S.op("pe", lambda e, j=j, c=c, ptv=ptv: e.transpose(out=ptv[:, j, :], in_=xn[:, c:c + 128],
                                                                    identity=ident),
                     reads=[xn_key, "ident"], writes=pk)
            wsl = sm[:, nwbase + 4 * g:nwbase + 4 * g + 4].unsqueeze(2).broadcast_to([128, 4, 128])
            S.op("dve", lambda e, g=g, ptv=ptv, wsl=wsl: e.tensor_tensor(out=dst4(g), in0=ptv, in1=wsl,
                                                                         op=ALU.mult),
                 reads=pk + ["sm"], writes=[dst_key])

    def norm_tile(src_dram, xt, xt_key, xn, xn_key, nwbase, dst4, dst_key, load=True):
        norm_a(src_dram, xt, xt_key, xn, xn_key, load)
        norm_b(xn, xn_key, nwbase, dst4, dst_key)

    def rope(e_out, xin, xin_keys, H, C, Sg, cs_key, t1, t2, tkey):
        xv = xin.rearrange("p h (a j c) -> p h a j c", a=2, j=2, c=32)
        t2v = t2.rearrange("p h (a j c) -> p h a j c", a=2, j=2, c=32)
        Sv = Sg.rearrange("p (a j c) -> p a j c", a=2, j=2, c=32)
        Cb = C.unsqueeze(1).broadcast_to([128, H, 128])
        S.op("dve", lambda e: e.tensor_tensor(out=t1, in0=xin, in1=Cb, op=ALU.mult),
             reads=xin_keys + [cs_key], writes=[(tkey, 1)])
        for j in range(2):
            sb = Sv[:, :, j, :].unsqueeze(1).broadcast_to([128, H, 2, 32])
            S.op("dve", lambda e, j=j, sb=sb: e.tensor_tensor(
                out=t2v[:, :, :, j, :], in0=xv[:, :, :, 1 - j, :], in1=sb, op=ALU.mult),
                reads=xin_keys + [cs_key], writes=[(tkey, 2)])
        S.op("dve", lambda e: e.tensor_tensor(out=e_out[0], in0=t1, in1=t2, op=ALU.add),
             reads=[(tkey, 1), (tkey, 2)], writes=[e_out[1]])

    def qknorm_rope(outb, psv, pkeys, H, goff, cs, cs_key, tA, tB, tU, tCg, tSg, tT, tkey):
        s4, sk = next_sc()
        S.op("act", lambda e: e.activation(out=tA, in_=psv, func=AF.Square), reads=pkeys, writes=[(tkey, "A")])
        S.op("dve", lambda e: e.tensor_reduce(out=s4[:, 0:H], in_=tA, axis=AX.X, op=ALU.add),
             reads=[(tkey, "A")], writes=[sk])
        S.op("act", lambda e: e.activation(out=s4[:, 0:H], in_=s4[:, 0:H], func=AF.Sqrt, scale=1.0 / 128,
                                           bias=epsc), reads=[sk] + CST, writes=[sk])
        S.op("dve", lambda e: e.reciprocal(out=s4[:, 0:H], in_=s4[:, 0:H]), reads=[sk], writes=[sk])
        S.op("dve", lambda e: e.tensor_tensor(out=tCg, in0=cs[:, 0:128], in1=sm[:, goff:goff + 128],
                                              op=ALU.mult), reads=[cs_key, "sm"], writes=[(tkey, "Cg")])
        S.op("dve", lambda e: e.tensor_tensor(out=tSg, in0=cs[:, 128:256], in1=sm[:, goff + 128:goff + 256],
                                              op=ALU.mult), reads=[cs_key, "sm"], writes=[(tkey, "Sg")])
        rb = s4[:, 0:H].unsqueeze(2).broadcast_to([128, H, 128])
        S.op("dve", lambda e: e.tensor_tensor(out=tU, in0=psv, in1=rb, op=ALU.mult),
             reads=pkeys + [sk], writes=[(tkey, "U")])
        rope(outb, tU, [(tkey, "U"), (tkey, "Cg"), (tkey, "Sg")], H, tCg, tSg, (tkey, "Cg"), tT, tB, tkey)

    def load_w(dst, dst_keys, src):
        S.dma("pool", out=dst, in_=src, writes=dst_keys)

    def tap(name, ap_sb, keys):
        if debug is not None and debug[0] == name:
            S.dma("pool", out=dbg_d, in_=ap_sb, reads=list(keys))

    def kv_alloc(A):
        B = {}
        B["wkv"] = A.take([16, 512], BF16)
        B["kr"] = [A.take([2, 128], BF16) for _ in range(2)]
        B["kst"] = [A.take([2, 128], BF16) for _ in range(2)]
        B["vst"] = [A.take([256], BF16) for _ in range(2)]
        for nm in ("tA", "tB", "tU", "tT"):
            B[nm] = A.take([2, 128], F32)
        B["tC"] = A.take([128], F32)
        B["tS"] = A.take([128], F32)
        return B

    def kv_a(B, tag, lhs_fn, hkeys, cs, cs_key, kt):
        sl = kt % 2
        pa, pk = next_mm()
        for dc in range(DC):
            S.op("pe", lambda e, dc=dc: e.matmul(pa, lhsT=lhs_fn(dc), rhs=B["wkv"][:, dc, :], start=(dc == 0),
                                                 stop=(dc == DC - 1)), reads=hkeys + [(tag, "wkv")], writes=pk)
        S.op("act", lambda e: e.activation(out=B["vst"][sl], in_=pa[:, 256:512], func=AF.Copy),
             reads=pk, writes=[(tag, "vst", sl)])
        S.dma("pool", out=vsc[:, kt, :], in_=B["vst"][sl], reads=[(tag, "vst", sl)], writes=[("vsc", kt)])
        pv = pa[:, 0:256].rearrange("p (h c) -> p h c", h=2, c=128)
        qknorm_rope((B["kr"][sl], (tag, "kr", sl)), pv, pk, 2, SM_GQ + 256, cs, cs_key,
                    B["tA"], B["tB"], B["tU"], B["tC"], B["tS"], B["tT"], (tag, "t"))

    def kv_b(B, tag, kt):
        sl = kt % 2
        pt, ptk = next_tr()
        ptv = pt.rearrange("p (a b) -> p a b", a=4, b=128)
        for j in range(2):
            S.op("pe", lambda e, j=j: e.transpose(out=ptv[:, j, :], in_=B["kr"][sl][:, j, :], identity=ident),
                 reads=[(tag, "kr", sl), "ident"], writes=ptk)
        S.op("act", lambda e: e.activation(out=B["kst"][sl], in_=ptv[:, 0:2, :], func=AF.Copy),
             reads=ptk, writes=[(tag, "kst", sl)])
        S.dma("pool", out=ksc[:, :, kt * 128:(kt + 1) * 128], in_=B["kst"][sl], reads=[(tag, "kst", sl)],
              writes=[("ksc", kt)])

    A0 = Alloc(PBASE)
    Sacc = A0.take([8, 256], F32)
    masks = A0.take([8, 128], F32)
    A0BASE = A0.off
    e_base = SM_EO
    WO = A0.take([2, NO, 8], F32)
    A0K = A0.off
    mA = A0.take([128], F32)
    mB = A0.take([128], F32)
    for h in range(NRH):
        S.op("act", lambda e, h=h: e.activation(out=mA, in_=sm[:, SM_REL:SM_REL + 128], func=AF.Exp,
                                                scale=lg[:, h:h + 1], bias=lnsc),
             reads=["sm", "lg"] + CST, writes=["mA"])
        S.op("act", lambda e, h=h: e.activation(out=mB, in_=sm[:, SM_REL + 128:SM_REL + 256], func=AF.Exp,
                                                scale=lg[:, 8 + h:9 + h], bias=lnsc),
             reads=["sm", "lg"] + CST, writes=["mB"])
        S.op("dve", lambda e, h=h: e.tensor_tensor(out=masks[:, h, :], in0=mA, in1=mB, op=ALU.add),
             reads=["mA", "mB"], writes=[("mask", h)])
    eo = sm[:, e_base:e_base + 2 * NO].rearrange("p (t d) -> p t d", t=NO, d=2)
    for d_ in range(2):
        for h in range(NRH):
            S.op("act", lambda e, d_=d_, h=h: e.activation(
                out=WO[:, d_, :, h], in_=eo[:, :, d_], func=AF.Exp, scale=lg[:, d_ * 8 + h:d_ * 8 + h + 1],
                bias=lnsc), reads=["sm", "lg"] + CST, writes=["WO"])
    S.op("dve", lambda e: e.memset(Sacc, 0.0), writes=["Sacc"])
    Wk0 = A0.take([16, 512], BF16)
    Wv0 = A0.take([16, 512], BF16)
    mxt = A0.take([2, D], F32)
    mxn = A0.take([D], BF16)
    mT = A0.take([16, 256], BF16)
    load_w(Wk0, ["Wk0"], w_ck_v)
    load_w(Wv0, ["Wv0"], w_cv_v)
    for j in range(2):
        norm_tile(mem_d[j * 128:(j + 1) * 128, :], mxt[:, j, :], ("mxt", j), mxn, "mxn", SM_NW + 32,
                  lambda g, j=j: mT[:, 4 * g:4 * g + 4, j * 128:(j + 1) * 128], ("mT", j))
    for hd in range(4):
        pa, pk = next_mm()
        for dc in range(DC):
            S.op("pe", lambda e, hd=hd, dc=dc, pa=pa: e.matmul(
                pa[:, 0:256], lhsT=Wk0[:, dc, hd * 128:(hd + 1) * 128], rhs=mT[:, dc, :],
                start=(dc == 0), stop=(dc == DC - 1)), reads=["Wk0", ("mT", 0), ("mT", 1)], writes=pk)
        S.op("act", lambda e, hd=hd, pa=pa: e.activation(out=KcT[:, hd, :], in_=pa[:, 0:256], func=AF.Copy),
             reads=pk, writes=["KcT"])
    for j in range(2):
        pa, pk = next_mm()
        for dc in range(DC):
            S.op("pe", lambda e, j=j, dc=dc, pa=pa: e.matmul(
                pa, lhsT=mT[:, dc, j * 128:(j + 1) * 128], rhs=Wv0[:, dc, :],
                start=(dc == 0), stop=(dc == DC - 1)), reads=["Wv0", ("mT", j)], writes=pk)
        S.op("act", lambda e, j=j, pa=pa: e.activation(out=Vc[:, j, :], in_=pa, func=AF.Copy),
             reads=pk, writes=["Vc"])
    S.barrier()


    if stop == 0:
        S.finish(); S.emit(nc, stack); stack.close(); return nc
    A1 = Alloc(A0K)
    Wres = [A1.take([16, 512], BF16) for _ in range(4)]
    xt1 = [A1.take([D], F32) for _ in range(2)]
    xn1 = [A1.take([D], BF16) for _ in range(2)]
    hTt = [A1.take([16, 128], BF16) for _ in range(2)]
    cs1 = [A1.take([256], F32) for _ in range(2)]
    kr1 = [A1.take([2, 128], BF16) for _ in range(3)]
    V21 = [A1.take([2, 256], BF16) for _ in range(3)]
    t1a = [A1.take([2, 128], F32) for _ in range(3)]
    t1b = [A1.take([2, 128], F32) for _ in range(3)]
    KB1 = kv_alloc(A1)
    load_w(KB1["wkv"], [("kv1", "wkv")], w_in_v[:, :, 5120:5632])
    for i in range(4):
        load_w(Wres[i][:, :, 0:256], [("Wres", i, 0)], w_in_v[:, :, 1024 + 256 * i:1024 + 256 * i + 256])
        load_w(Wres[i][:, :, 256:512], [("Wres", i, 1)], w_in_v[:, :, 2048 + 256 * i:2048 + 256 * i + 256])
    cnt = 0

    def p1_na(ot):
        sl = ot % 2
        norm_a(x_oth[ot * 128:(ot + 1) * 128, :], xt1[sl], ("xt1", sl), xn1[sl], ("xn1", sl))

    def p1_nb(ot):
        sl = ot % 2
        norm_b(xn1[sl], ("xn1", sl), SM_NW, lambda g, sl=sl: hTt[sl][:, 4 * g:4 * g + 4, :], ("hTt", sl))
        S.dma("sp", out=cs1[sl], in_=cs_oth[ot * 128:(ot + 1) * 128, :], writes=[("cs1", sl)])

    def p1_state(ot, i, s2):
        pc, pck = ps_full(6 + (s2 % 2))
        for hh in range(2):
            S.op("pe", lambda e, hh=hh, pc=pc, s2=s2: e.matmul(
                pc[:, hh * 256:(hh + 1) * 256], lhsT=kr1[s2][:, hh, :], rhs=V21[s2][:, hh, :],
                start=True, stop=True), reads=[("kr1", s2), ("V21", s2)], writes=pck)
        sv = Sacc[:, 2 * i:2 * i + 2, :].rearrange("p a b -> p (a b)")
        S.op("dve", lambda e, pc=pc, sv=sv: e.tensor_tensor(out=sv, in0=pc, in1=sv, op=ALU.add),
             reads=pck + ["Sacc"], writes=["Sacc"])

    p1_na(0)
    p1_na(1)
    p1_nb(0)
    pendq = []
    for ot in range(NO):
        sl = ot % 2
        if ot + 2 < NO:
            p1_na(ot + 2)
        if ot + 1 < NO:
            p1_nb(ot + 1)
        for i in range(4):
            pa, pk = next_mm()
            for dc in range(DC):
                S.op("pe", lambda e, dc=dc, pa=pa, sl=sl, i=i: e.matmul(
                    pa, lhsT=hTt[sl][:, dc, :], rhs=Wres[i][:, dc, :], start=(dc == 0), stop=(dc == DC - 1)),
                    reads=[("hTt", sl), ("Wres", i, 0), ("Wres", i, 1)], writes=pk)
            if len(pendq) >= 2:
                p1_state(*pendq.pop(0))
            s2 = cnt % 3
            cnt += 1
            pv = pa[:, 0:256].rearrange("p (h c) -> p h c", h=2, c=128)
            rope((kr1[s2], ("kr1", s2)), pv, pk, 2, cs1[sl][:, 0:128], cs1[sl][:, 128:256], ("cs1", sl),
                 t1a[s2], t1b[s2], ("t1", s2))
            for hh in range(2):
                for d_ in range(2):
                    S.op("act", lambda e, hh=hh, d_=d_, pa=pa, s2=s2, ot=ot, i=i: e.activation(
                        out=V21[s2][:, hh, d_ * 128:(d_ + 1) * 128], in_=pa[:, 256 + hh * 128:384 + hh * 128],
                        func=AF.Copy, scale=WO[:, d_, ot, 2 * i + hh:2 * i + hh + 1]),
                        reads=pk + ["WO"], writes=[("V21", s2)])
            pendq.append((ot, i, s2))
        kv_a(KB1, "kv1", lambda dc, sl=sl: hTt[sl][:, dc, :], [("hTt", sl)], cs1[sl], ("cs1", sl), ot)
        if ot >= 1:
            kv_b(KB1, "kv1", ot - 1)
    while pendq:
        p1_state(*pendq.pop(0))
    kv_b(KB1, "kv1", NO - 1)
    tap("sacc", Sacc, ["Sacc"])
    S.barrier()


    if stop == 1:
        S.finish(); S.emit(nc, stack); stack.close(); return nc
    A2 = Alloc(A0BASE)
    hTo = A2.take([16, TOK], BF16)
    A2K = A2.off
    xt2 = [A2.take([D], F32) for _ in range(2)]
    xn2 = [A2.take([D], BF16) for _ in range(2)]
    KB2 = kv_alloc(A2)
    cs2a = [A2.take([256], F32) for _ in range(2)]
    load_w(KB2["wkv"], [("kv2", "wkv")], w_in_v[:, :, 5120:5632])
    for t in range(NT + 3):
        if t < NT:
            sl = t % 2
            norm_a(x_own[t * 128:(t + 1) * 128, :], xt2[sl], ("xt2", sl), xn2[sl], ("xn2", sl))
        if 0 <= t - 1 < NT:
            t0_ = t - 1
            norm_b(xn2[t0_ % 2], ("xn2", t0_ % 2), SM_NW,
                   lambda g, t0_=t0_: hTo[:, 4 * g:4 * g + 4, t0_ * 128:(t0_ + 1) * 128], ("hTo", t0_))
            S.dma("sp", out=cs2a[t0_ % 2], in_=cs_own[t0_ * 128:(t0_ + 1) * 128, :], writes=[("cs2a", t0_ % 2)])
        if 0 <= t - 2 < NT:
            t1_ = t - 2
            kv_a(KB2, "kv2", lambda dc, t1_=t1_: hTo[:, dc, t1_ * 128:(t1_ + 1) * 128], [("hTo", t1_)],
                 cs2a[t1_ % 2], ("cs2a", t1_ % 2), NO + t1_)
        if 0 <= t - 3 < NT:
            kv_b(KB2, "kv2", NO + t - 3)
    tap("hTo", hTo, [("hTo", t) for t in range(NT)])
    S.barrier()
    A2 = Alloc(A2K)
    Wb2 = [A2.take([16, 512], BF16) for _ in range(2)]
    QKT = A2.take([NT, 2, 128], BF16)
    Vst = A2.take([NT, 128], BF16)
    Gst = A2.take([NT, 128], BF16)
    dS = A2.take([NT, 256], F32)
    Stb = A2.take([NT, 256], BF16)
    YTh = A2.take([TOK], BF16)
    Sfb = A2.take([2, 128], F32)
    cs2 = [A2.take([256], F32) for _ in range(2)]
    gnw = [A2.take([2, 128], F32) for _ in range(2)]
    qk_r = [A2.take([2, 128], BF16) for _ in range(2)]
    V22 = [A2.take([256], BF16) for _ in range(2)]
    AT = [A2.take([128], BF16) for _ in range(2)]
    tf_ = [A2.take([128], F32) for _ in range(2)]
    o_ = [A2.take([128], F32) for _ in range(2)]
    yb = [A2.take([128], BF16) for _ in range(2)]
    t2a = [A2.take([2, 128], F32) for _ in range(2)]
    t2b = [A2.take([2, 128], F32) for _ in range(2)]
    bst2 = [A2.take([8], F32) for _ in range(2)]
    def make_head(h):
        wb = Wb2[h % 2]
        wk = ("Wb2", h % 2)
        gs = h % 2
        wkeys = [(wk, j) for j in range(4)]

        def setup():
            for j in range(4):
                load_w(wb[:, :, j * 128:(j + 1) * 128], [(wk, j)],
                       w_in_v[:, :, j * 1024 + h * 128:j * 1024 + h * 128 + 128])
            S.dma("sp", out=gnw[gs][:, 0, :], in_=gnwb_d[:, h * 128:(h + 1) * 128], writes=[("gnw", gs)])
            S.dma("sp", out=gnw[gs][:, 1, :], in_=gnwb_d[:, 1024 + h * 128:1024 + (h + 1) * 128],
                  writes=[("gnw", gs)])

        def l1_a(t, h=h, wb=wb, wkeys=wkeys, gs=gs):
            sl = t % 2
            S.dma("sp", out=cs2[sl], in_=cs_own[t * 128:(t + 1) * 128, :], writes=[("cs2", sl)])
            pa, pk = next_mm()
            for dc in range(DC):
                S.op("pe", lambda e, dc=dc: e.matmul(
                    pa, lhsT=hTo[:, dc, t * 128:(t + 1) * 128], rhs=wb[:, dc, :], start=(dc == 0),
                    stop=(dc == DC - 1)), reads=[("hTo", t)] + wkeys, writes=pk)
            S.op("act", lambda e: e.activation(out=V22[sl][:, 0:128], in_=pa[:, 256:384], func=AF.Copy,
                                               scale=WKF[:, h:h + 1]), reads=pk + TABK, writes=[("V22", sl)])
            S.op("act", lambda e: e.activation(out=V22[sl][:, 128:256], in_=pa[:, 256:384], func=AF.Copy,
                                               scale=WKB[:, h:h + 1]), reads=pk + TABK, writes=[("V22", sl)])
            S.op("act", lambda e: e.activation(out=Vst[:, t, :], in_=pa[:, 256:384], func=AF.Copy),
                 reads=pk, writes=[("Vst", t)])
            S.op("act", lambda e: e.activation(out=Gst[:, t, :], in_=pa[:, 384:512], func=AF.Silu),
                 reads=pk, writes=[("Gst", t)])
            pv = pa[:, 0:256].rearrange("p (h c) -> p h c", h=2, c=128)
            rope((qk_r[sl], ("qk_r", sl)), pv, pk, 2, cs2[sl][:, 0:128], cs2[sl][:, 128:256], ("cs2", sl),
                 t2a[sl], t2b[sl], ("t2", sl))

        def l1_b(t, h=h, wb=wb, wkeys=wkeys, gs=gs):
            sl = t % 2
            pt, ptk = next_tr()
            ptv = pt.rearrange("p (a b) -> p a b", a=4, b=128)
            for j in range(2):
                S.op("pe", lambda e, j=j: e.transpose(out=ptv[:, j, :], in_=qk_r[sl][:, j, :], identity=ident),
                     reads=[("qk_r", sl), "ident"], writes=ptk)
            S.op("act", lambda e: e.activation(out=QKT[:, t, :, :], in_=ptv[:, 0:2, :], func=AF.Copy),
                 reads=ptk, writes=[("QKT", t)])
            pc, pck = next_mm()
            S.op("pe", lambda e: e.matmul(pc[:, 0:256], lhsT=qk_r[sl][:, 1, :], rhs=V22[sl], start=True, stop=True),
                 reads=[("qk_r", sl), ("V22", sl)], writes=pck)
            S.op("act", lambda e: e.activation(out=dS[:, t, :], in_=pc[:, 0:256], func=AF.Copy),
                 reads=pck, writes=[("dS", t)])

        def l1_step(k):
            if k < NT:
                l1_a(k)
            if k >= 1:
                l1_b(k - 1)

        def scan():
            scan_body()

        def scan_body():
            pass
        def scan_real():
            S.op("dve", lambda e, h=h: e.tensor_copy(out=Sfb[:, 0, :], in_=Sacc[:, h, 0:128]), reads=["Sacc"],
                 writes=["Sf"])
            S.op("dve", lambda e, h=h: e.tensor_copy(out=Sfb[:, 1, :], in_=Sacc[:, h, 128:256]), reads=["Sacc"],
                 writes=["Sb"])
            for t in range(NT):
                S.op("act", lambda e, t=t: e.activation(out=Stb[:, t, 0:128], in_=Sfb[:, 0, :], func=AF.Copy),
                     reads=["Sf"], writes=[("Stb", t, 0)])
                if t < NT - 1:
                    S.op("dve", lambda e, t=t, h=h: e.scalar_tensor_tensor(
                        out=Sfb[:, 0, :], in0=Sfb[:, 0, :], scalar=dcy[:, h:h + 1], in1=dS[:, t, 0:128],
                        op0=ALU.mult, op1=ALU.add), reads=["Sf", ("dS", t), "dcy"], writes=["Sf"])
                tb = NT - 1 - t
                S.op("act", lambda e, tb=tb: e.activation(out=Stb[:, tb, 128:256], in_=Sfb[:, 1, :], func=AF.Copy),
                     reads=["Sb"], writes=[("Stb", tb, 1)])
                if tb > 0:
                    S.op("dve", lambda e, tb=tb, h=h: e.scalar_tensor_tensor(
                        out=Sfb[:, 1, :], in0=Sfb[:, 1, :], scalar=dcy[:, 8 + h:9 + h], in1=dS[:, tb, 128:256],
                        op0=ALU.mult, op1=ALU.add), reads=["Sb", ("dS", tb), "dcy"], writes=["Sb"])


        def l2_a(t, h=h, wb=wb, wkeys=wkeys, gs=gs):
            sl = t % 2
            pa, pk = next_mm()
            l2st[t] = (pa, pk)
            S.op("pe", lambda e: e.matmul(pa[:, 0:128], lhsT=QKT[:, t, 1, :], rhs=QKT[:, t, 0, :],
                                          start=True, stop=True), reads=[("QKT", t)], writes=pk)
            S.op("pe", lambda e: e.matmul(pa[:, 256:512], lhsT=QKT[:, t, 0, :], rhs=Stb[:, t, :],
                                          start=True, stop=True),
                 reads=[("QKT", t), ("Stb", t, 0), ("Stb", t, 1)], writes=pk)
            S.op("dve", lambda e: e.tensor_tensor(out=AT[sl], in0=pa[:, 0:128], in1=masks[:, h, :], op=ALU.mult),
                 reads=pk + [("mask", h)], writes=[("AT", sl)])
            S.op("act", lambda e: e.activation(out=tf_[sl], in_=pa[:, 256:384], func=AF.Copy, scale=DQF[:, h:h + 1]),
                 reads=pk + TABK, writes=[("tf", sl)])
            S.op("dve", lambda e: e.scalar_tensor_tensor(
                out=o_[sl], in0=pa[:, 384:512], scalar=DQB[:, h:h + 1], in1=tf_[sl], op0=ALU.mult, op1=ALU.add),
                reads=pk + TABK + [("tf", sl)], writes=[("o", sl)])

        def l2_b(t, h=h, wb=wb, wkeys=wkeys, gs=gs):
            sl = t % 2
            bs = bst2[sl]
            bk = ("bst", sl)
            pb, pbk = ps_full(6 + (t % 2))
            S.op("pe", lambda e: e.matmul(pb[:, 0:128], lhsT=AT[sl], rhs=Vst[:, t, :], start=True, stop=True),
                 reads=[("AT", sl), ("Vst", t)], writes=pbk)
            S.op("dve", lambda e: e.tensor_tensor(out=o_[sl], in0=pb[:, 0:128], in1=o_[sl], op=ALU.add),
                 reads=pbk + [("o", sl)], writes=[("o", sl)])
            S.op("dve", lambda e: e.bn_stats(out=bs[:, 0:6], in_=o_[sl]), reads=[("o", sl)], writes=[bk])
            S.op("dve", lambda e: e.bn_aggr(out=bs[:, 6:8], in_=bs[:, 0:6]), reads=[bk], writes=[bk])
            S.op("act", lambda e: e.activation(out=bs[:, 7:8], in_=bs[:, 7:8], func=AF.Sqrt, bias=epsc),
                 reads=[bk] + CST, writes=[bk])
            S.op("dve", lambda e: e.reciprocal(out=bs[:, 7:8], in_=bs[:, 7:8]), reads=[bk], writes=[bk])
            S.op("dve", lambda e: e.tensor_scalar(out=o_[sl], in0=o_[sl], scalar1=bs[:, 6:7], scalar2=bs[:, 7:8],
                                                  op0=ALU.subtract, op1=ALU.mult),
                 reads=[bk, ("o", sl)], writes=[("o", sl)])
            S.op("dve", lambda e: e.tensor_tensor(out=o_[sl], in0=o_[sl], in1=gnw[gs][:, 0, :], op=ALU.mult),
                 reads=[("o", sl), ("gnw", gs)], writes=[("o", sl)])
            S.op("dve", lambda e: e.tensor_tensor(out=o_[sl], in0=o_[sl], in1=gnw[gs][:, 1, :], op=ALU.add),
                 reads=[("o", sl), ("gnw", gs)], writes=[("o", sl)])
            S.op("dve", lambda e: e.tensor_tensor(out=yb[sl], in0=o_[sl], in1=Gst[:, t, :], op=ALU.mult),
                 reads=[("o", sl), ("Gst", t)], writes=[("yb", sl)])

        def l2_c(t, h=h, wb=wb, wkeys=wkeys, gs=gs):
            sl = t % 2
            pt, ptk = next_tr()
            S.op("pe", lambda e: e.transpose(out=pt[:, 0:128], in_=yb[sl], identity=ident),
                 reads=[("yb", sl), "ident"], writes=ptk)
            S.op("act", lambda e: e.activation(out=YTh[:, t * 128:(t + 1) * 128], in_=pt[:, 0:128], func=AF.Copy),
                 reads=ptk, writes=["YTh"])

        l2st = {}

        def l2_step(k):
            if k < NT:
                l2_a(k)
            if 0 <= k - 1 < NT:
                l2_b(k - 1)
            if 0 <= k - 2 < NT:
                l2_c(k - 2)

        def finish():
            S.dma("sp", out=ytr[:, h, :], in_=YTh, reads=["YTh"], writes=[("ytr", h)])
            if debug is not None and debug[0] == "yth%d" % h:
                tap("yth%d" % h, YTh, ["YTh"])

        return dict(setup=setup, l1_step=l1_step, scan=scan_real, l2_step=l2_step, finish=finish)

    HD = [make_head(h) for h in range(NRH)]
    HD[0]["setup"]()
    for k in range(NT + 1):
        HD[0]["l1_step"](k)
    HD[0]["scan"]()
    for h in range(NRH):
        nxt = HD[h + 1] if h + 1 < NRH else None
        if nxt is not None:
            nxt["setup"]()
        for k in range(NT + 4):
            if k < NT + 2:
                HD[h]["l2_step"](k)
            if nxt is not None and 0 <= k - 3 <= NT:
                nxt["l1_step"](k - 3)
        HD[h]["finish"]()
        if nxt is not None:
            nxt["scan"]()
    S.barrier()


    if stop == 2:
        S.finish(); S.emit(nc, stack); stack.close(); return nc
    A3 = Alloc(PBASE)
    KT = A3.take([2, NK * 128], BF16)
    Vt = A3.take([NK, 256], BF16)
    A3K = A3.off
    for kvh in range(2):
        S.dma("sp", out=KT[:, kvh, :], in_=ksc[:, kvh, :], writes=[("KT", kvh)])
    NQ = 4 if NK >= 4 else 1
    for q4 in range(NQ):
        a, b = q4 * NK // NQ, (q4 + 1) * NK // NQ
        S.dma("sp", out=Vt[:, a:b, :], in_=vsc[:, a:b, :], writes=[("Vt", q4)])
    tap("kt", KT, [("KT", 0), ("KT", 1)])
    tap("vt", Vt, [("Vt", q4) for q4 in range(NQ)])
    S.barrier()
    KVK = [("KT", kt) for kt in range(NK)] + [("Vt", kt) for kt in range(NK)]


    if stop == 3:
        S.finish(); S.emit(nc, stack); stack.close(); return nc
    A4 = Alloc(A3K)
    xg = A4.take([GT, D], F32)
    hTg = A4.take([16, GW], BF16)
    NPG = 24
    R = A4.take([NPG, GW], BF16)
    Wb4 = [A4.take([16, 512], BF16) for _ in range(2)]
    xn4 = A4.take([D], BF16)
    PT2 = [A4.take([2 * GW], BF16) for _ in range(2)]
    rec = [A4.take([GW], F32) for _ in range(2)]
    cs4 = [A4.take([256], F32) for _ in range(2)]
    qr4 = A4.take([4, 128], BF16)
    t4A = A4.take([4, 128], F32)
    t4B = A4.take([4, 128], F32)
    t4U = A4.take([4, 128], F32)
    t4C = A4.take([128], F32)
    t4S = A4.take([128], F32)
    t4T = A4.take([4, 128], F32)
    Pc = [A4.take([256], BF16) for _ in range(2)]
    PnT = [A4.take([2, 128], BF16) for _ in range(2)]
    ocT = [A4.take([4, 128], BF16)] * 2
    rl = [A4.take([GW], F32) for _ in range(2)]
    RK = lambda a, b: [("R", p) for p in range(a, b)]
    wfin_v = R[:, 16:24, :].rearrange("p a b -> p (a b)").bitcast(F32)
    wcnt = [0]

    def next_w():
        k = wcnt[0] % 2
        wcnt[0] += 1
        return Wb4[k], ("Wb4", k)

    for g in range(NG):
        tok0 = g * GW
        for i in range(GT):
            t = g * GT + i
            norm_tile(x_own[t * 128:(t + 1) * 128, :], xg[:, i, :], ("xg", i), xn4, "xn4", SM_NW,
                      lambda gq, i=i: hTg[:, 4 * gq:4 * gq + 4, i * 128:(i + 1) * 128], ("hTg", i))
        for blk in range(2):
            wb, wk = next_w()
            load_w(wb, [wk], w_in_v[:, :, 4096 + blk * 512:4096 + (blk + 1) * 512])
            for i in range(GT):
                t = g * GT + i
                sl = i % 2
                S.dma("sp", out=cs4[sl], in_=cs_own[t * 128:(t + 1) * 128, :], writes=[("cs4", sl)])
                pa, pk = next_mm()
                for dc in range(DC):
                    S.op("pe", lambda e, dc=dc, pa=pa, i=i, wb=wb: e.matmul(
                        pa, lhsT=hTg[:, dc, i * 128:(i + 1) * 128], rhs=wb[:, dc, :], start=(dc == 0),
                        stop=(dc == DC - 1)), reads=[("hTg", i), wk], writes=pk)
                pv = pa.rearrange("p (h c) -> p h c", h=4, c=128)
                qknorm_rope((qr4, "qr4"), pv, pk, 4, SM_GQ, cs4[sl], ("cs4", sl), t4A, t4B, t4U, t4C, t4S, t4T, "t4")
                for hp in range(2):
                    pt, ptk = next_tr()
                    ptv = pt.rearrange("p (a b) -> p a b", a=4, b=128)
                    for j in range(2):
                        S.op("pe", lambda e, j=j, hp=hp, ptv=ptv: e.transpose(
                            out=ptv[:, j, :], in_=qr4[:, 2 * hp + j, :], identity=ident),
                            reads=["qr4", "ident"], writes=ptk)
                    h0 = blk * 4 + 2 * hp
                    S.op("act", lambda e, ptv=ptv, h0=h0, i=i: e.activation(
                        out=R[:, h0:h0 + 2, i * 128:(i + 1) * 128], in_=ptv[:, 0:2, :], func=AF.Copy),
                        reads=ptk, writes=RK(h0, h0 + 2))
        its = [(h, kp) for h in range(8) for kp in range(NK // 2)]
        SK = 1
        for idx in range(len(its) + SK):
            if idx < len(its):
                h, kp = its[idx]
                kvh = h // 4
                bp = 1 + idx % 2
                pkeys = [("ps", 2 * bp), ("ps", 2 * bp + 1)]
                ps_ = idx % 2
                for u in range(2):
                    kt = 2 * kp + u
                    S.op("pe", lambda e, bp=bp, u=u, kt=kt, kvh=kvh, h=h: e.matmul(
                        psd[bp][:, u * 512:(u + 1) * 512], lhsT=KT[:, kvh, kt * 128:(kt + 1) * 128], rhs=R[:, h, :],
                        start=True, stop=True), reads=[("R", h)], writes=[pkeys[u]])
                S.op("act", lambda e, bp=bp, ps_=ps_: e.activation(out=PT2[ps_], in_=psd[bp][:, :], func=AF.Exp,
                                                                  scale=ISQ, bias=negc),
                     reads=pkeys + ["negc"], writes=[("PT2", ps_)])
            j = idx - SK
            if j >= 0:
                h, kp = its[j]
                kvh = h // 4
                ps_ = j % 2
                bo, bz = (6, 7) if h % 2 == 0 else (0, 1)
                po, pok = ps_full(bo)
                pz, pzk = ps_full(bz)
                for u in range(2):
                    kt = 2 * kp + u
                    S.op("pe", lambda e, po=po, kt=kt, kvh=kvh, ps_=ps_, u=u: e.matmul(
                        po, lhsT=Vt[:, kt, kvh * 128:(kvh + 1) * 128], rhs=PT2[ps_][:, u * 512:(u + 1) * 512],
                        start=(kt == 0), stop=(kt == NK - 1)), reads=[("PT2", ps_)], writes=pok)
                    S.op("pe", lambda e, pz=pz, kt=kt, ps_=ps_, u=u: e.matmul(
                        pz, lhsT=ones, rhs=PT2[ps_][:, u * 512:(u + 1) * 512], start=(kt == 0), stop=(kt == NK - 1)),
                        reads=[("PT2", ps_), "ones"], writes=pzk)
                if kp == NK // 2 - 1:
                    rc = rec[h % 2]
                    S.op("dve", lambda e, pz=pz, rc=rc: e.reciprocal(out=rc, in_=pz), reads=pzk,
                         writes=[("rec", h % 2)])
                    S.op("dve", lambda e, po=po, h=h, rc=rc: e.tensor_tensor(out=R[:, 16 + h, :], in0=po, in1=rc,
                                                                             op=ALU.mult),
                         reads=pok + [("rec", h % 2)], writes=[("R", 16 + h)])
        S.dma("sp", out=R[:, 8:16, :], in_=ytr[:, :, tok0:tok0 + GW], reads=[("ytr", h) for h in range(8)],
              writes=RK(8, 16))
        if g == 0:
            tap("yt", R[:, 8:24, :], RK(8, 24))
            tap("qt", R[:, 0:8, :], RK(0, 8))
        for c in range(4):
            wb, wk = next_w()
            load_w(wb, [wk], w_out_v[:, :, c * 512:(c + 1) * 512])
            for i in range(GT):
                pa, pk = next_mm()
                for fc in range(16):
                    S.op("pe", lambda e, fc=fc, pa=pa, i=i, wb=wb: e.matmul(
                        pa, lhsT=R[:, 8 + fc, i * 128:(i + 1) * 128], rhs=wb[:, fc, :], start=(fc == 0),
                        stop=(fc == 15)), reads=[("R", 8 + fc), wk], writes=pk)
                xv = xg[:, i, c * 512:(c + 1) * 512]
                S.op("dve", lambda e, pa=pa, xv=xv: e.tensor_tensor(out=xv, in0=pa, in1=xv, op=ALU.add),
                     reads=pk + [("xg", i)], writes=[("xg", i)])
        if g == 0:
            tap("xg1", xg, [("xg", i) for i in range(GT)])
        for i in range(GT):
            norm_tile(None, xg[:, i, :], ("xg", i), xn4, "xn4", SM_NW + 16,
                      lambda gq, i=i: hTg[:, 4 * gq:4 * gq + 4, i * 128:(i + 1) * 128], ("hTg", i), load=False)
        wb, wk = next_w()
        load_w(wb, [wk], w_cq_v)
        for hd in range(4):
            pa, pk = next_mm()
            for dc in range(DC):
                S.op("pe", lambda e, dc=dc, pa=pa, hd=hd, wb=wb: e.matmul(
                    pa[:, 0:GW], lhsT=wb[:, dc, hd * 128:(hd + 1) * 128], rhs=hTg[:, dc, :], start=(dc == 0),
                    stop=(dc == DC - 1)), reads=[("hTg", i) for i in range(GT)] + [wk], writes=pk)
            S.op("act", lambda e, pa=pa, hd=hd: e.activation(out=R[:, hd, :], in_=pa[:, 0:GW], func=AF.Copy),
                 reads=pk, writes=[("R", hd)])
        wb, wk = next_w()
        wco = wb.rearrange("p a b -> p (a b)").rearrange("p (a b) -> p a b", a=4, b=2048)
        load_w(wco, [wk], w_co_v)
        cits = [(i, hd) for i in range(GT) for hd in range(4)]
        cst_ = {}

        def ca_A(k):
            i, hd = cits[k]
            pa, pk = next_mm()
            s4, sk = next_sc()
            sl = k % 2
            S.op("pe", lambda e: e.matmul(pa[:, 0:256], lhsT=R[:, hd, i * 128:(i + 1) * 128], rhs=KcT[:, hd, :],
                                          start=True, stop=True), reads=[("R", hd), "KcT"], writes=pk)
            S.op("dve", lambda e: e.tensor_reduce(out=s4[:, 0:1], in_=pa[:, 0:256], axis=AX.X, op=ALU.max),
                 reads=pk, writes=[sk])
            S.op("dve", lambda e: e.tensor_scalar(out=s4[:, 0:1], in0=s4[:, 0:1], scalar1=-ISQ, scalar2=None,
                                                  op0=ALU.mult), reads=[sk], writes=[sk])
            S.op("act", lambda e: e.activation(out=Pc[sl], in_=pa[:, 0:256], func=AF.Exp, scale=ISQ,
                                               bias=s4[:, 0:1], accum_out=s4[:, 1:2]),
                 reads=pk + [sk], writes=[("Pc", sl), sk])
            S.op("dve", lambda e: e.reciprocal(out=s4[:, 2:3], in_=s4[:, 1:2]), reads=[sk], writes=[sk])
            S.op("dve", lambda e: e.tensor_scalar(out=Pc[sl], in0=Pc[sl], scalar1=s4[:, 2:3], scalar2=None,
                                                  op0=ALU.mult), reads=[("Pc", sl), sk], writes=[("Pc", sl)])

        def ca_B(k):
            sl = k % 2
            pt, ptk = next_tr()
            ptv = pt.rearrange("p (a b) -> p a b", a=4, b=128)
            for j in range(2):
                S.op("pe", lambda e, j=j: e.transpose(out=ptv[:, j, :], in_=Pc[sl][:, j * 128:(j + 1) * 128],
                                                      identity=ident), reads=[("Pc", sl), "ident"], writes=ptk)
            S.op("act", lambda e: e.activation(out=PnT[sl], in_=ptv[:, 0:2, :], func=AF.Copy),
                 reads=ptk, writes=[("PnT", sl)])

        def ca_C(k, wco=wco, wk=wk):
            i, hd = cits[k]
            sl = k % 2
            osl = i % 2
            pa, pk = next_mm()
            for j in range(2):
                S.op("pe", lambda e, j=j: e.matmul(pa[:, 0:128], lhsT=Vc[:, j, hd * 128:(hd + 1) * 128],
                                                   rhs=PnT[sl][:, j, :], start=(j == 0), stop=(j == 1)),
                     reads=[("PnT", sl), "Vc"], writes=pk)
            S.op("act", lambda e: e.activation(out=ocT[osl][:, hd, :], in_=pa[:, 0:128], func=AF.Copy),
                 reads=pk, writes=[("ocT", 0, hd)])
            if hd == 3:
                for c in range(4):
                    pb, pbk = next_mm()
                    for h2 in range(4):
                        S.op("pe", lambda e, h2=h2, c=c, pb=pb: e.matmul(
                            pb, lhsT=ocT[osl][:, h2, :], rhs=wco[:, h2, c * 512:(c + 1) * 512], start=(h2 == 0),
                            stop=(h2 == 3)), reads=[("ocT", 0, h2), wk], writes=pbk)
                    xv = xg[:, i, c * 512:(c + 1) * 512]
                    S.op("dve", lambda e, pb=pb, xv=xv: e.tensor_tensor(out=xv, in0=pb, in1=xv, op=ALU.add),
                         reads=pbk + [("xg", i)], writes=[("xg", i)])

        for k in range(len(cits) + 2):
            if k < len(cits):
                ca_A(k)
            if 0 <= k - 1 < len(cits):
                ca_B(k - 1)
            if 0 <= k - 2 < len(cits):
                ca_C(k - 2)
        if g == 0:
            tap("xg2", xg, [("xg", i) for i in range(GT)])
        for i in range(GT):
            norm_tile(None, xg[:, i, :], ("xg", i), xn4, "xn4", SM_NW + 48,
                      lambda gq, i=i: hTg[:, 4 * gq:4 * gq + 4, i * 128:(i + 1) * 128], ("hTg", i), load=False)
        HK = [("hTg", i) for i in range(GT)]
        ucnt = 0
        for qf in range(4):
            for ub in range(4):
                wb, wk = next_w()
                c0 = qf * 2048 + ub * 512
                load_w(wb, [wk], w_up_v[:, :, c0:c0 + 512])
                for j in range(4):
                    f = ub * 4 + j
                    b = (0, 1, 6, 7)[ucnt % 4]
                    pa, pk = ps_full(b)
                    for dc in range(DC):
                        S.op("pe", lambda e, dc=dc, pa=pa, j=j, wb=wb: e.matmul(
                            pa[:, 0:GW], lhsT=wb[:, dc, j * 128:(j + 1) * 128], rhs=hTg[:, dc, :], start=(dc == 0),
                            stop=(dc == DC - 1)), reads=HK + [wk], writes=pk)
                    rs_ = ucnt % 2
                    ucnt += 1
                    S.op("act", lambda e, pa=pa, rs_=rs_: e.activation(out=rl[rs_], in_=pa[:, 0:GW], func=AF.Relu),
                         reads=pk, writes=[("rl", rs_)])
                    S.op("dve", lambda e, rs_=rs_, f=f: e.tensor_tensor(out=R[:, f, :], in0=rl[rs_], in1=rl[rs_],
                                                                       op=ALU.mult),
                         reads=[("rl", rs_)], writes=[("R", f)])
            for c in range(4):
                wb, wk = next_w()
                load_w(wb, [wk], w_dn_v[:, qf * 16:(qf + 1) * 16, c * 512:(c + 1) * 512])
                for i in range(GT):
                    pa, pk = ps_full(2 + i)
                    for f in range(16):
                        S.op("pe", lambda e, f=f, pa=pa, i=i, wb=wb: e.matmul(
                            pa, lhsT=R[:, f, i * 128:(i + 1) * 128], rhs=wb[:, f, :], start=(f == 0),
                            stop=(f == 15)), reads=[("R", f), wk], writes=pk)
                    xv = xg[:, i, c * 512:(c + 1) * 512]
                    S.op("dve", lambda e, pa=pa, xv=xv: e.tensor_tensor(out=xv, in0=pa, in1=xv, op=ALU.add),
                         reads=pk + [("xg", i)], writes=[("xg", i)])
        if g == 0:
            tap("xg3", xg, [("xg", i) for i in range(GT)])
        S.dma("sp", out=wfin_v, in_=wfin_d, writes=RK(16, 24))
        for i in range(GT):
            t = g * GT + i
            s4, sk = next_sc()
            S.op("act", lambda e, i=i, s4=s4: e.activation(out=xn4, in_=xg[:, i, :], func=AF.Square,
                                                          accum_out=s4[:, 0:1]),
                 reads=[("xg", i)], writes=["xn4", sk])
            S.op("act", lambda e, s4=s4: e.activation(out=s4[:, 1:2], in_=s4[:, 0:1], func=AF.Sqrt, scale=1.0 / D,
                                                     bias=epsc), reads=[sk] + CST, writes=[sk])
            S.op("dve", lambda e, s4=s4: e.reciprocal(out=s4[:, 2:3], in_=s4[:, 1:2]), reads=[sk], writes=[sk])
            S.op("dve", lambda e, i=i, s4=s4: e.scalar_tensor_tensor(
                out=xg[:, i, :], in0=xg[:, i, :], scalar=s4[:, 2:3], in1=wfin_v, op0=ALU.mult, op1=ALU.mult),
                reads=[("xg", i), sk] + RK(16, 24), writes=[("xg", i)])
            S.dma("sp", out=out_d[t * 128:(t + 1) * 128, :], in_=xg[:, i, :], reads=[("xg", i)],
                  writes=[("out", t)])
    S.finish()
    S.emit(nc, stack)
    stack.close()
    return nc


def _rope_cs(seq):
    rows = seq // 64
    row = np.repeat(np.arange(rows, dtype=np.float32), 64)
    col = np.tile(np.arange(64, dtype=np.float32), rows)
    inv = (1.0 / (np.float32(10000.0) ** (np.arange(0, 64, 2, dtype=np.float32) / np.float32(64)))).astype(np.float32)
    ar = (row[:, None] * inv[None, :]).astype(np.float32)
    ac = (col[:, None] * inv[None, :]).astype(np.float32)
    cr, sr, cc, sc_ = np.cos(ar), np.sin(ar), np.cos(ac), np.sin(ac)
    C = np.concatenate([cr, cr, cc, cc], axis=1)
    Sg = np.concatenate([-sr, sr, -sc_, sc_], axis=1)
    return np.concatenate([C, Sg], axis=1).astype(np.float32)


def _swap32(g):
    v = g.reshape(2, 2, 32)
    return v[:, ::-1, :].reshape(128)


_NC_CACHE = {}


def kernel(x, mem, norm_mix_w, w_in, ret_decay_fwd, ret_decay_bwd, ret_gn_w, ret_gn_b,
           attn_q_norm_w, attn_k_norm_w, w_out, norm_cross_w, norm_mem_w,
           w_cross_q, w_cross_k, w_cross_v, w_cross_o, norm_mlp_w,
           w_mlp_up, w_mlp_down, norm_final_w, _debug=None):
    f = lambda a: np.ascontiguousarray(np.asarray(a, dtype=np.float32))
    x = f(x)
    mem = f(mem)
    B, SEQ, _ = x.shape
    TOK = SEQ // 4
    NT = TOK // 128
    NO = 3 * NT
    key = (SEQ, None if _debug is None else _debug[0])
    if key not in _NC_CACHE:
        _NC_CACHE[key] = build(SEQ, _debug)
    nc = _NC_CACHE[key]
    cs = _rope_cs(SEQ)
    idx = np.arange(128, dtype=np.float32)
    relf = np.where(idx[None, :] >= idx[:, None], idx[None, :] - idx[:, None], BIG).astype(np.float32)
    relb = np.where(idx[:, None] > idx[None, :], idx[:, None] - idx[None, :], BIG).astype(np.float32)
    nw = np.concatenate([f(w).reshape(-1)[:D].reshape(16, 128).T for w in
                         (norm_mix_w, norm_cross_w, norm_mem_w, norm_mlp_w)], axis=1)
    rep = lambda v: np.broadcast_to(f(v).reshape(1, -1), (128, f(v).size))
    gq = f(attn_q_norm_w).reshape(128)
    gk = f(attn_k_norm_w).reshape(128)
    common = {
        "ident": np.eye(128, dtype=np.float32),
        "gnwb": np.ascontiguousarray(np.concatenate([rep(ret_gn_w), rep(ret_gn_b)], axis=1)),
        "wfin": np.ascontiguousarray(rep(norm_final_w)),
        "w_in": f(w_in)[0], "w_out": f(w_out)[0], "w_cq": f(w_cross_q)[0], "w_ck": f(w_cross_k)[0],
        "w_cv": f(w_cross_v)[0], "w_co": f(w_cross_o)[0], "w_up": f(w_mlp_up)[0], "w_dn": f(w_mlp_down)[0],
    }
    in_maps = []
    for c in range(8):
        b, j = c // 4, c % 4
        own = slice(j * TOK, (j + 1) * TOK)
        oth_idx = np.concatenate([np.arange(s * TOK, (s + 1) * TOK) for s in range(4) if s != j])
        ef = np.where(oth_idx < j * TOK, j * TOK - 1 - oth_idx, BIG).astype(np.float32)
        eb = np.where(oth_idx >= (j + 1) * TOK, oth_idx - (j + 1) * TOK, BIG).astype(np.float32)
        eo = np.stack([ef.reshape(NO, 128).T, eb.reshape(NO, 128).T], axis=2).reshape(128, 2 * NO)
        sm = np.concatenate([
            nw, rep(ret_decay_fwd), rep(ret_decay_bwd),
            np.stack([idx + 1, 128 - idx, 127 - idx, idx], axis=1),
            rep(gq), rep(_swap32(gq)), rep(gk), rep(_swap32(gk)),
            relf, relb, eo], axis=1).astype(np.float32)
        m = dict(common)
        m.update({
            "x_own": np.ascontiguousarray(x[b, own]),
            "x_oth": np.ascontiguousarray(x[b, oth_idx]),
            "cs_own": np.ascontiguousarray(cs[own]),
            "cs_oth": np.ascontiguousarray(cs[oth_idx]),
            "smalls": np.ascontiguousarray(sm),
            "mem": np.ascontiguousarray(mem[b]),
        })
        in_maps.append(m)
    res = run_bass_kernel_spmd(nc, in_maps, core_ids=list(range(8)))
    out = np.empty((B, SEQ, D), dtype=np.float32)
    for c in range(8):
        b, j = c // 4, c % 4
        out[b, j * TOK:(j + 1) * TOK] = res.results[c]["out"]
    if _debug is not None:
        return out, [res.results[c]["dbg"] for c in range(8)]
    return out
```

```python
import math
from contextlib import ExitStack

import numpy as np
import concourse.bass as bass
import concourse.mybir as mybir
from concourse.bass_utils import run_bass_kernel_spmd

F32 = mybir.dt.float32
BF16 = mybir.dt.bfloat16
AF = mybir.ActivationFunctionType
ALU = mybir.AluOpType
AX = mybir.AxisListType

COMPUTE = ("pe", "act", "dve", "pool")
ENGS = ("pe", "act", "dve", "pool", "sp")
EIDX = {e: i for i, e in enumerate(COMPUTE)}

D = 2048
DC = 16
DFF = 8192
NRH = 8
EPS = 1e-6
BIG = 1.0e9
LNSC = -0.5 * math.log(128.0)
ISQ = 128.0 ** -0.5


class _Op:
    __slots__ = ("fn", "waits", "inc", "dma")

    def __init__(self, fn, dma=None):
        self.fn = fn
        self.waits = []
        self.inc = False
        self.dma = dma


class Sched:
    def __init__(self, nslots=16):
        self.ops = {e: [] for e in ENGS}
        self.clock = {e: [-1] * len(COMPUTE) for e in ENGS}
        self.snap = {e: [] for e in ENGS}
        self.known_d = {e: {} for e in ENGS}
        self.last_w = {}
        self.readers = {}
        self.nslots = nslots
        self.slot_target = {}
        self.slot_rr = {q: 0 for q in ENGS}
        self.dma_pending = []
        self.last_real = {e: -1 for e in COMPUTE}

    def _need(self, eng, ev, op):
        if ev[0] == "c":
            _, x, i = ev
            xi = EIDX[x]
            if self.clock[eng][xi] >= i:
                return
            op.waits.append(ev)
            self.ops[x][i].inc = True
            sn = self.snap[x][i]
            ck = self.clock[eng]
            for k in range(len(COMPUTE)):
                if sn[k] > ck[k]:
                    ck[k] = sn[k]
            if ck[xi] < i:
                ck[xi] = i
        else:
            _, slot, target, sn = ev
            if self.known_d[eng].get(slot, 0) >= target:
                return
            op.waits.append(ev)
            self.known_d[eng][slot] = target
            if sn is not None:
                ck = self.clock[eng]
                for k in range(len(COMPUTE)):
                    if sn[k] > ck[k]:
                        ck[k] = sn[k]

    def _deps(self, eng, op, reads, writes, is_dma):
        for r in reads:
            w = self.last_w.get(r)
            if w is not None:
                if not (w[0] == "c" and w[1] == eng and not is_dma and eng == "pe"):
                    self._need(eng, w, op)
            if isinstance(r, tuple) and r[0] == "ps":
                rd = self.readers.get(r)
                if rd:
                    for x, i in list(rd[0].items()):
                        if x != eng:
                            self._need(eng, ("c", x, i), op)
        for r in writes:
            w = self.last_w.get(r)
            if w is not None:
                same = w[0] == "c" and w[1] == eng and not is_dma and eng == "pe"
                if not same:
                    self._need(eng, w, op)
            rd = self.readers.get(r)
            if rd:
                for x, i in rd[0].items():
                    if x == eng and not is_dma and eng == "pe":
                        continue
                    self._need(eng, ("c", x, i), op)
                for dev in rd[1]:
                    self._need(eng, dev, op)

    def _commit(self, ev, reads, writes):
        ws = set(writes)
        for r in writes:
            self.last_w[r] = ev
            self.readers[r] = [{}, []]
        for r in reads:
            if r in ws:
                continue
            rd = self.readers.setdefault(r, [{}, []])
            if ev[0] == "c":
                if rd[0].get(ev[1], -1) < ev[2]:
                    rd[0][ev[1]] = ev[2]
            else:
                rd[1].append(ev)

    def op(self, eng, fn, reads=(), writes=()):
        o = _Op(fn)
        idx = len(self.ops[eng])
        self._deps(eng, o, reads, writes, False)
        self.ops[eng].append(o)
        self.snap[eng].append(tuple(self.clock[eng]))
        self.last_real[eng] = idx
        self._commit(("c", eng, idx), reads, writes)
        return o

    def dma(self, q, out, in_, reads=(), writes=(), **kw):
        o = _Op(None, dma=True)
        self._deps(q, o, reads, writes, True)
        slot = (q, self.slot_rr[q] % self.nslots)
        self.slot_rr[q] += 1
        prev = self.slot_target.get(slot, 0)
        if prev and self.known_d[q].get(slot, 0) < prev:
            o.waits.append(("d", slot, prev, None))
            self.known_d[q][slot] = prev
        target = prev + 16
        self.slot_target[slot] = target
        o.dma = (slot, out, in_, kw)
        self.ops[q].append(o)
        sn = tuple(self.clock[q])
        self.snap[q].append(sn)
        ev = ("d", slot, target, sn)
        self._commit(ev, reads, writes)
        self.dma_pending.append(ev)
        return o

    def barrier(self):
        evs = [("c", x, self.last_real[x]) for x in COMPUTE if self.last_real[x] >= 0]
        devs = list(self.dma_pending)
        self.dma_pending = []
        for e in ENGS:
            o = _Op(None)
            for ev in evs:
                if ev[1] == e and e == "pe":
                    continue
                self._need(e, ev, o)
            for ev in devs:
                self._need(e, ev, o)
            self.ops[e].append(o)
            self.snap[e].append(tuple(self.clock[e]))
        self.last_w = {}
        self.readers = {}

    def finish(self):
        o = _Op(None)
        for ev in self.dma_pending:
            self._need("sp", ev, o)
        self.ops["sp"].append(o)
        self.snap["sp"].append(tuple(self.clock["sp"]))

    def emit(self, nc, stack):
        sems = {e: stack.enter_context(nc.semaphore("c_" + e)) for e in COMPUTE}
        slot_sem = {}
        for slot in self.slot_target:
            slot_sem[slot] = stack.enter_context(nc.semaphore("d_%s_%d" % slot))
        prefix = {}
        for x in COMPUTE:
            c = 0
            p = []
            for o in self.ops[x]:
                if o.inc:
                    c += 1
                p.append(c)
            prefix[x] = p
        block = stack.enter_context(nc.Block())
        ops = self.ops

        def run(engname):
            def body(e):
                for o in ops[engname]:
                    for w in o.waits:
                        if w[0] == "c":
                            e.wait_ge(sems[w[1]], prefix[w[1]][w[2]])
                        else:
                            e.wait_ge(slot_sem[w[1]], w[2])
                    if o.dma:
                        slot, out, in_, kw = o.dma
                        e.dma_start(out=out, in_=in_, **kw).then_inc(slot_sem[slot], 16)
                    elif o.fn is not None:
                        ins = o.fn(e)
                        if o.inc:
                            ins.then_inc(sems[engname], 1)
            return body

        if ops["pe"]:
            block.tensor(run("pe"))
        if ops["act"]:
            block.scalar(run("act"))
        if ops["dve"]:
            block.vector(run("dve"))
        if ops["pool"]:
            block.gpsimd(run("pool"))
        if ops["sp"]:
            block.sync(run("sp"))


SM_NW = 0
SM_DEC = 64
SM_IDX = 80
SM_GQ = 84
SM_REL = 84 + 512
SM_EO = SM_REL + 256


def build(SEQ, debug=None, stop=None):
    TOK = SEQ // 4
    NT = TOK // 128
    NO = 3 * NT
    NK = 4 * NT
    GT = 4 if NT >= 4 else NT
    NG = NT // GT
    GW = GT * 128
    NSM = SM_EO + 2 * NO

    nc = bass.Bass("TRN2", target_bir_lowering=False)

    def din(name, shape, dt=F32):
        return nc.dram_tensor(name, list(shape), dt, kind="ExternalInput").ap()

    x_own = din("x_own", [TOK, D])
    x_oth = din("x_oth", [3 * TOK, D])
    cs_own = din("cs_own", [TOK, 256])
    cs_oth = din("cs_oth", [3 * TOK, 256])
    smalls = din("smalls", [128, NSM])
    ident_d = din("ident", [128, 128])
    gnwb_d = din("gnwb", [128, 2048])
    wfin_d = din("wfin", [128, D])
    mem_d = din("mem", [256, D])
    w_in = din("w_in", [D, 5632])
    w_out = din("w_out", [D, D])
    w_cq = din("w_cq", [D, 512])
    w_ck = din("w_ck", [D, 512])
    w_cv = din("w_cv", [D, 512])
    w_co = din("w_co", [512, D])
    w_up = din("w_up", [D, DFF])
    w_dn = din("w_dn", [DFF, D])
    out_d = nc.dram_tensor("out", [TOK, D], F32, kind="ExternalOutput").ap()
    ytr = nc.dram_tensor("ytr", [128, 8, TOK], BF16).ap()
    ksc = nc.dram_tensor("ksc", [128, 2, NK * 128], BF16).ap()
    vsc = nc.dram_tensor("vsc", [128, NK, 256], BF16).ap()
    dbg_d = None
    if debug is not None:
        dbg_d = nc.dram_tensor("dbg", list(debug[1]), F32, kind="ExternalOutput").ap()

    w_in_v = w_in.rearrange("(dc p) n -> p dc n", p=128)
    w_out_v = w_out.rearrange("(dc p) n -> p dc n", p=128)
    w_cq_v = w_cq.rearrange("(dc p) n -> p dc n", p=128)
    w_ck_v = w_ck.rearrange("(dc p) n -> p dc n", p=128)
    w_cv_v = w_cv.rearrange("(dc p) n -> p dc n", p=128)
    w_co_v = w_co.rearrange("(dc p) n -> p dc n", p=128)
    w_up_v = w_up.rearrange("(dc p) n -> p dc n", p=128)
    w_dn_v = w_dn.rearrange("(f p) n -> p f n", p=128)

    S = Sched()
    stack = ExitStack()
    ARENA_BYTES = 212832
    arena = stack.enter_context(nc.sbuf_tensor("arena", [128, ARENA_BYTES // 4], F32))
    psb = [stack.enter_context(nc.psum_tensor("psb%d" % i, [128, 512], F32)) for i in range(8)]

    class Alloc:
        def __init__(self, base):
            self.off = base

        def take(self, shape, dt):
            n = 1
            for s in shape:
                n *= s
            nb = n * (2 if dt == BF16 else 4)
            nb = (nb + 63) // 64 * 64
            off = self.off
            self.off += nb
            assert self.off <= ARENA_BYTES, ("SBUF arena overflow", self.off)
            a = arena[:, off // 4:(off + nb) // 4]
            if dt == BF16:
                a = a.bitcast(BF16)
            a = a[:, 0:n]
            if len(shape) == 2:
                a = a.rearrange("p (a b) -> p a b", a=shape[0], b=shape[1])
            elif len(shape) == 3:
                a = a.rearrange("p (a b c) -> p a b c", a=shape[0], b=shape[1], c=shape[2])
            return a

    def ps_full(b):
        return psb[b][:, :], [("ps", b)]

    def ps_half_bf(b, h):
        a = psb[b][:, h * 256:(h + 1) * 256].bitcast(BF16)
        return a, [("ps", b)]

    rot = {"mm": 0, "tr": 0}

    def next_mm():
        b = 2 + rot["mm"] % 4
        rot["mm"] += 1
        return ps_full(b)

    def next_tr():
        k = rot["tr"] % 2
        rot["tr"] += 1
        return ps_half_bf(k, 0)

    P = Alloc(0)
    ident = P.take([128], BF16)
    ones = P.take([128], BF16)
    sm = P.take([NSM], F32)
    lg = P.take([16], F32)
    dcy = P.take([16], F32)
    tabs = P.take([4, 8], F32)
    cst = P.take([8], F32)
    KcT = P.take([4, 256], BF16)
    Vc = P.take([2, 512], BF16)
    sc = P.take([16, 4], F32)
    PBASE = P.off

    lnsc = cst[:, 0:1]
    epsc = cst[:, 1:2]
    zero = cst[:, 2:3]
    negc = cst[:, 3:4]
    scn = [0]

    def next_sc():
        k = scn[0] % 16
        scn[0] += 1
        return sc[:, k, :], ("sc", k)

    S.dma("sp", out=sm, in_=smalls, writes=["sm"])
    S.dma("pool", out=ident, in_=ident_d, writes=["ident"])
    S.op("dve", lambda e: e.memset(ones, 1.0), writes=["ones"])
    S.op("dve", lambda e: e.memset(cst[:, 0:1], LNSC), writes=["cst0"])
    S.op("dve", lambda e: e.memset(cst[:, 1:2], EPS), writes=["cst1"])
    S.op("dve", lambda e: e.memset(cst[:, 2:3], 0.0), writes=["cst2"])
    CST = ["cst0", "cst1", "cst2"]
    S.op("act", lambda e: e.activation(out=lg, in_=sm[:, SM_DEC:SM_DEC + 16], func=AF.Exp),
         reads=["sm"], writes=["lg"])
    S.op("dve", lambda e: e.tensor_scalar(out=lg, in0=lg, scalar1=-1.0, scalar2=None, op0=ALU.mult),
         reads=["lg"], writes=["lg"])
    S.op("act", lambda e: e.activation(out=dcy, in_=lg, func=AF.Exp, scale=128.0),
         reads=["lg"], writes=["dcy"])
    for k in range(4):
        d0 = 0 if k in (0, 2) else 8
        S.op("dve", lambda e, k=k, d0=d0: e.tensor_scalar(
            out=tabs[:, k, :], in0=lg[:, d0:d0 + 8], scalar1=sm[:, SM_IDX + k:SM_IDX + k + 1],
            scalar2=None, op0=ALU.mult), reads=["lg", "sm"], writes=[("tab", k)])
        S.op("act", lambda e, k=k: e.activation(
            out=tabs[:, k, :], in_=tabs[:, k, :], func=AF.Exp, bias=(lnsc if k >= 2 else zero)),
            reads=[("tab", k)] + CST, writes=[("tab", k)])
    DQF, DQB, WKF, WKB = (tabs[:, k, :] for k in range(4))
    TABK = [("tab", k) for k in range(4)]
    S.op("dve", lambda e: e.tensor_reduce(out=cst[:, 4:5], in_=sm[:, SM_GQ:SM_GQ + 128], axis=AX.X,
                                          op=ALU.max, apply_absolute_value=True), reads=["sm"], writes=["cst4"])
    S.op("dve", lambda e: e.tensor_reduce(out=cst[:, 5:6], in_=sm[:, SM_GQ + 256:SM_GQ + 384], axis=AX.X,
                                          op=ALU.max, apply_absolute_value=True), reads=["sm"], writes=["cst5"])
    S.op("dve", lambda e: e.tensor_tensor(out=cst[:, 3:4], in0=cst[:, 4:5], in1=cst[:, 5:6], op=ALU.mult),
         reads=["cst4", "cst5"], writes=["negc"])
    S.op("dve", lambda e: e.tensor_scalar(out=cst[:, 3:4], in0=cst[:, 3:4], scalar1=-math.sqrt(128.0),
                                          scalar2=None, op0=ALU.mult), reads=["negc"], writes=["negc"])

    def norm_a(src_dram, xt, xt_key, xn, xn_key, load=True):
        if load:
            S.dma("sp", out=xt, in_=src_dram, writes=[xt_key])
        s4, sk = next_sc()
        S.op("act", lambda e: e.activation(out=xn, in_=xt, func=AF.Square, accum_out=s4[:, 0:1]),
             reads=[xt_key], writes=[xn_key, sk])
        S.op("act", lambda e: e.activation(out=s4[:, 1:2], in_=s4[:, 0:1], func=AF.Sqrt,
                                           scale=1.0 / D, bias=epsc), reads=[sk] + CST, writes=[sk])
        S.op("dve", lambda e: e.reciprocal(out=s4[:, 2:3], in_=s4[:, 1:2]), reads=[sk], writes=[sk])
        S.op("dve", lambda e: e.tensor_scalar(out=xn, in0=xt, scalar1=s4[:, 2:3], scalar2=None,
                                              op0=ALU.mult), reads=[xt_key, sk], writes=[xn_key])

    def norm_b(xn, xn_key, nwbase, dst4, dst_key):
        for g in range(4):
            pt, pk = next_tr()
            ptv = pt.rearrange("p (a b) -> p a b", a=4, b=128)
            for j in range(4):
                c = (4 * g + j) * 128
                S.op("pe", lambda e, j=j, c=c, ptv=ptv: e.transpose(out=ptv[:, j, :], in_=xn[:, c:c + 128],
                                                                    identity=ident),
                     reads=[xn_key, "ident"], writes=pk)
            wsl = sm[:, nwbase + 4 * g:nwbase + 4 * g + 4].unsqueeze(2).broadcast_to([128, 4, 128])
            S.op("dve", lambda e, g=g, ptv=ptv, wsl=wsl: e.tensor_tensor(out=dst4(g), in0=ptv, in1=wsl,
                                                                         op=ALU.mult),
                 reads=pk + ["sm"], writes=[dst_key])

    def norm_tile(src_dram, xt, xt_key, xn, xn_key, nwbase, dst4, dst_key, load=True):
        norm_a(src_dram, xt, xt_key, xn, xn_key, load)
        norm_b(xn, xn_key, nwbase, dst4, dst_key)

    def rope(e_out, xin, xin_keys, H, C, Sg, cs_key, t1, t2, tkey):
        xv = xin.rearrange("p h (a j c) -> p h a j c", a=2, j=2, c=32)
        t2v = t2.rearrange("p h (a j c) -> p h a j c", a=2, j=2, c=32)
        Sv = Sg.rearrange("p (a j c) -> p a j c", a=2, j=2, c=32)
        Cb = C.unsqueeze(1).broadcast_to([128, H, 128])
        S.op("dve", lambda e: e.tensor_tensor(out=t1, in0=xin, in1=Cb, op=ALU.mult),
             reads=xin_keys + [cs_key], writes=[(tkey, 1)])
        for j in range(2):
            sb = Sv[:, :, j, :].unsqueeze(1).broadcast_to([128, H, 2, 32])
            S.op("dve", lambda e, j=j, sb=sb: e.tensor_tensor(
                out=t2v[:, :, :, j, :], in0=xv[:, :, :, 1 - j, :], in1=sb, op=ALU.mult),
                reads=xin_keys + [cs_key], writes=[(tkey, 2)])
        S.op("dve", lambda e: e.tensor_tensor(out=e_out[0], in0=t1, in1=t2, op=ALU.add),
             reads=[(tkey, 1), (tkey, 2)], writes=[e_out[1]])

    def qknorm_rope(outb, psv, pkeys, H, goff, cs, cs_key, tA, tB, tU, tCg, tSg, tT, tkey):
        s4, sk = next_sc()
        S.op("act", lambda e: e.activation(out=tA, in_=psv, func=AF.Square), reads=pkeys, writes=[(tkey, "A")])
        S.op("dve", lambda e: e.tensor_reduce(out=s4[:, 0:H], in_=tA, axis=AX.X, op=ALU.add),
             reads=[(tkey, "A")], writes=[sk])
        S.op("act", lambda e: e.activation(out=s4[:, 0:H], in_=s4[:, 0:H], func=AF.Sqrt, scale=1.0 / 128,
                                           bias=epsc), reads=[sk] + CST, writes=[sk])
        S.op("dve", lambda e: e.reciprocal(out=s4[:, 0:H], in_=s4[:, 0:H]), reads=[sk], writes=[sk])
        S.op("dve", lambda e: e.tensor_tensor(out=tCg, in0=cs[:, 0:128], in1=sm[:, goff:goff + 128],
                                              op=ALU.mult), reads=[cs_key, "sm"], writes=[(tkey, "Cg")])
        S.op("dve", lambda e: e.tensor_tensor(out=tSg, in0=cs[:, 128:256], in1=sm[:, goff + 128:goff + 256],
                                              op=ALU.mult), reads=[cs_key, "sm"], writes=[(tkey, "Sg")])
        rb = s4[:, 0:H].unsqueeze(2).broadcast_to([128, H, 128])
        S.op("dve", lambda e: e.tensor_tensor(out=tU, in0=psv, in1=rb, op=ALU.mult),
             reads=pkeys + [sk], writes=[(tkey, "U")])
        rope(outb, tU, [(tkey, "U"), (tkey, "Cg"), (tkey, "Sg")], H, tCg, tSg, (tkey, "Cg"), tT, tB, tkey)

    def load_w(dst, dst_keys, src):
        S.dma("pool", out=dst, in_=src, writes=dst_keys)

    def tap(name, ap_sb, keys):
        if debug is not None and debug[0] == name:
            S.dma("pool", out=dbg_d, in_=ap_sb, reads=list(keys))

    def kv_alloc(A):
        B = {}
        B["wkv"] = A.take([16, 512], BF16)
        B["kr"] = [A.take([2, 128], BF16) for _ in range(2)]
        B["kst"] = [A.take([2, 128], BF16) for _ in range(2)]
        B["vst"] = [A.take([256], BF16) for _ in range(2)]
        for nm in ("tA", "tB", "tU", "tT"):
            B[nm] = A.take([2, 128], F32)
        B["tC"] = A.take([128], F32)
        B["tS"] = A.take([128], F32)
        return B

    def kv_a(B, tag, lhs_fn, hkeys, cs, cs_key, kt):
        sl = kt % 2
        pa, pk = next_mm()
        for dc in range(DC):
            S.op("pe", lambda e, dc=dc: e.matmul(pa, lhsT=lhs_fn(dc), rhs=B["wkv"][:, dc, :], start=(dc == 0),
                                                 stop=(dc == DC - 1)), reads=hkeys + [(tag, "wkv")], writes=pk)
        S.op("act", lambda e: e.activation(out=B["vst"][sl], in_=pa[:, 256:512], func=AF.Copy),
             reads=pk, writes=[(tag, "vst", sl)])
        S.dma("pool", out=vsc[:, kt, :], in_=B["vst"][sl], reads=[(tag, "vst", sl)], writes=[("vsc", kt)])
        pv = pa[:, 0:256].rearrange("p (h c) -> p h c", h=2, c=128)
        qknorm_rope((B["kr"][sl], (tag, "kr", sl)), pv, pk, 2, SM_GQ + 256, cs, cs_key,
                    B["tA"], B["tB"], B["tU"], B["tC"], B["tS"], B["tT"], (tag, "t"))

    def kv_b(B, tag, kt):
        sl = kt % 2
        pt, ptk = next_tr()
        ptv = pt.rearrange("p (a b) -> p a b", a=4, b=128)
        for j in range(2):
            S.op("pe", lambda e, j=j: e.transpose(out=ptv[:, j, :], in_=B["kr"][sl][:, j, :], identity=ident),
                 reads=[(tag, "kr", sl), "ident"], writes=ptk)
        S.op("act", lambda e: e.activation(out=B["kst"][sl], in_=ptv[:, 0:2, :], func=AF.Copy),
             reads=ptk, writes=[(tag, "kst", sl)])
        S.dma("pool", out=ksc[:, :, kt * 128:(kt + 1) * 128], in_=B["kst"][sl], reads=[(tag, "kst", sl)],
              writes=[("ksc", kt)])

    A0 = Alloc(PBASE)
    Sacc = A0.take([8, 256], F32)
    masks = A0.take([8, 128], F32)
    A0BASE = A0.off
    e_base = SM_EO
    WO = A0.take([2, NO, 8], F32)
    A0K = A0.off
    mA = A0.take([128], F32)
    mB = A0.take([128], F32)
    for h in range(NRH):
        S.op("act", lambda e, h=h: e.activation(out=mA, in_=sm[:, SM_REL:SM_REL + 128], func=AF.Exp,
                                                scale=lg[:, h:h + 1], bias=lnsc),
             reads=["sm", "lg"] + CST, writes=["mA"])
        S.op("act", lambda e, h=h: e.activation(out=mB, in_=sm[:, SM_REL + 128:SM_REL + 256], func=AF.Exp,
                                                scale=lg[:, 8 + h:9 + h], bias=lnsc),
             reads=["sm", "lg"] + CST, writes=["mB"])
        S.op("dve", lambda e, h=h: e.tensor_tensor(out=masks[:, h, :], in0=mA, in1=mB, op=ALU.add),
             reads=["mA", "mB"], writes=[("mask", h)])
    eo = sm[:, e_base:e_base + 2 * NO].rearrange("p (t d) -> p t d", t=NO, d=2)
    for d_ in range(2):
        for h in range(NRH):
            S.op("act", lambda e, d_=d_, h=h: e.activation(
                out=WO[:, d_, :, h], in_=eo[:, :, d_], func=AF.Exp, scale=lg[:, d_ * 8 + h:d_ * 8 + h + 1],
                bias=lnsc), reads=["sm", "lg"] + CST, writes=["WO"])
    S.op("dve", lambda e: e.memset(Sacc, 0.0), writes=["Sacc"])
    Wk0 = A0.take([16, 512], BF16)
    Wv0 = A0.take([16, 512], BF16)
    mxt = A0.take([2, D], F32)
    mxn = A0.take([D], BF16)
    mT = A0.take([16, 256], BF16)
    load_w(Wk0, ["Wk0"], w_ck_v)
    load_w(Wv0, ["Wv0"], w_cv_v)
    for j in range(2):
        norm_tile(mem_d[j * 128:(j + 1) * 128, :], mxt[:, j, :], ("mxt", j), mxn, "mxn", SM_NW + 32,
                  lambda g, j=j: mT[:, 4 * g:4 * g + 4, j * 128:(j + 1) * 128], ("mT", j))
    for hd in range(4):
        pa, pk = next_mm()
        for dc in range(DC):
            S.op("pe", lambda e, hd=hd, dc=dc, pa=pa: e.matmul(
                pa[:, 0:256], lhsT=Wk0[:, dc, hd * 128:(hd + 1) * 128], rhs=mT[:, dc, :],
                start=(dc == 0), stop=(dc == DC - 1)), reads=["Wk0", ("mT", 0), ("mT", 1)], writes=pk)
        S.op("act", lambda e, hd=hd, pa=pa: e.activation(out=KcT[:, hd, :], in_=pa[:, 0:256], func=AF.Copy),
             reads=pk, writes=["KcT"])
    for j in range(2):
        pa, pk = next_mm()
        for dc in range(DC):
            S.op("pe", lambda e, j=j, dc=dc, pa=pa: e.matmul(
                pa, lhsT=mT[:, dc, j * 128:(j + 1) * 128], rhs=Wv0[:, dc, :],
                start=(dc == 0), stop=(dc == DC - 1)), reads=["Wv0", ("mT", j)], writes=pk)
        S.op("act", lambda e, j=j, pa=pa: e.activation(out=Vc[:, j, :], in_=pa, func=AF.Copy),
             reads=pk, writes=["Vc"])
    S.barrier()


    if stop == 0:
        S.finish(); S.emit(nc, stack); stack.close(); return nc
    A1 = Alloc(A0K)
    Wres = [A1.take([16, 512], BF16) for _ in range(4)]
    xt1 = [A1.take([D], F32) for _ in range(2)]
    xn1 = [A1.take([D], BF16) for _ in range(2)]
    hTt = [A1.take([16, 128], BF16) for _ in range(2)]
    cs1 = [A1.take([256], F32) for _ in range(2)]
    kr1 = [A1.take([2, 128], BF16) for _ in range(3)]
    V21 = [A1.take([2, 256], BF16) for _ in range(3)]
    t1a = [A1.take([2, 128], F32) for _ in range(3)]
    t1b = [A1.take([2, 128], F32) for _ in range(3)]
    KB1 = kv_alloc(A1)
    load_w(KB1["wkv"], [("kv1", "wkv")], w_in_v[:, :, 5120:5632])
    for i in range(4):
        load_w(Wres[i][:, :, 0:256], [("Wres", i, 0)], w_in_v[:, :, 1024 + 256 * i:1024 + 256 * i + 256])
        load_w(Wres[i][:, :, 256:512], [("Wres", i, 1)], w_in_v[:, :, 2048 + 256 * i:2048 + 256 * i + 256])
    cnt = 0

    def p1_na(ot):
        sl = ot % 2
        norm_a(x_oth[ot * 128:(ot + 1) * 128, :], xt1[sl], ("xt1", sl), xn1[sl], ("xn1", sl))

    def p1_nb(ot):
        sl = ot % 2
        norm_b(xn1[sl], ("xn1", sl), SM_NW, lambda g, sl=sl: hTt[sl][:, 4 * g:4 * g + 4, :], ("hTt", sl))
        S.dma("sp", out=cs1[sl], in_=cs_oth[ot * 128:(ot + 1) * 128, :], writes=[("cs1", sl)])

    def p1_state(ot, i, s2):
        pc, pck = ps_full(6 + (s2 % 2))
        for hh in range(2):
            S.op("pe", lambda e, hh=hh, pc=pc, s2=s2: e.matmul(
                pc[:, hh * 256:(hh + 1) * 256], lhsT=kr1[s2][:, hh, :], rhs=V21[s2][:, hh, :],
                start=True, stop=True), reads=[("kr1", s2), ("V21", s2)], writes=pck)
        sv = Sacc[:, 2 * i:2 * i + 2, :].rearrange("p a b -> p (a b)")
        S.op("dve", lambda e, pc=pc, sv=sv: e.tensor_tensor(out=sv, in0=pc, in1=sv, op=ALU.add),
             reads=pck + ["Sacc"], writes=["Sacc"])

    p1_na(0)
    p1_na(1)
    p1_nb(0)
    pendq = []
    for ot in range(NO):
        sl = ot % 2
        if ot + 2 < NO:
            p1_na(ot + 2)
        if ot + 1 < NO:
            p1_nb(ot + 1)
        for i in range(4):
            pa, pk = next_mm()
            for dc in range(DC):
                S.op("pe", lambda e, dc=dc, pa=pa, sl=sl, i=i: e.matmul(
                    pa, lhsT=hTt[sl][:, dc, :], rhs=Wres[i][:, dc, :], start=(dc == 0), stop=(dc == DC - 1)),
                    reads=[("hTt", sl), ("Wres", i, 0), ("Wres", i, 1)], writes=pk)
            if len(pendq) >= 2:
                p1_state(*pendq.pop(0))
            s2 = cnt % 3
            cnt += 1
            pv = pa[:, 0:256].rearrange("p (h c) -> p h c", h=2, c=128)
            rope((kr1[s2], ("kr1", s2)), pv, pk, 2, cs1[sl][:, 0:128], cs1[sl][:, 128:256], ("cs1", sl),
                 t1a[s2], t1b[s2], ("t1", s2))
            for hh in range(2):
                for d_ in range(2):
                    S.op("act", lambda e, hh=hh, d_=d_, pa=pa, s2=s2, ot=ot, i=i: e.activation(
                        out=V21[s2][:, hh, d_ * 128:(d_ + 1) * 128], in_=pa[:, 256 + hh * 128:384 + hh * 128],
                        func=AF.Copy, scale=WO[:, d_, ot, 2 * i + hh:2 * i + hh + 1]),
                        reads=pk + ["WO"], writes=[("V21", s2)])
            pendq.append((ot, i, s2))
        kv_a(KB1, "kv1", lambda dc, sl=sl: hTt[sl][:, dc, :], [("hTt", sl)], cs1[sl], ("cs1", sl), ot)
        if ot >= 1:
            kv_b(KB1, "kv1", ot - 1)
    while pendq:
        p1_state(*pendq.pop(0))
    kv_b(KB1, "kv1", NO - 1)
    tap("sacc", Sacc, ["Sacc"])
    S.barrier()


    if stop == 1:
        S.finish(); S.emit(nc, stack); stack.close(); return nc
    A2 = Alloc(A0BASE)
    hTo = A2.take([16, TOK], BF16)
    A2K = A2.off
    xt2 = [A2.take([D], F32) for _ in range(2)]
    xn2 = [A2.take([D], BF16) for _ in range(2)]
    KB2 = kv_alloc(A2)
    cs2a = [A2.take([256], F32) for _ in range(2)]
    load_w(KB2["wkv"], [("kv2", "wkv")], w_in_v[:, :, 5120:5632])
    for t in range(NT + 3):
        if t < NT:
            sl = t % 2
            norm_a(x_own[t * 128:(t + 1) * 128, :], xt2[sl], ("xt2", sl), xn2[sl], ("xn2", sl))
        if 0 <= t - 1 < NT:
            t0_ = t - 1
            norm_b(xn2[t0_ % 2], ("xn2", t0_ % 2), SM_NW,
                   lambda g, t0_=t0_: hTo[:, 4 * g:4 * g + 4, t0_ * 128:(t0_ + 1) * 128], ("hTo", t0_))
            S.dma("sp", out=cs2a[t0_ % 2], in_=cs_own[t0_ * 128:(t0_ + 1) * 128, :], writes=[("cs2a", t0_ % 2)])
        if 0 <= t - 2 < NT:
            t1_ = t - 2
            kv_a(KB2, "kv2", lambda dc, t1_=t1_: hTo[:, dc, t1_ * 128:(t1_ + 1) * 128], [("hTo", t1_)],
                 cs2a[t1_ % 2], ("cs2a", t1_ % 2), NO + t1_)
        if 0 <= t - 3 < NT:
            kv_b(KB2, "kv2", NO + t - 3)
    tap("hTo", hTo, [("hTo", t) for t in range(NT)])
    S.barrier()
    A2 = Alloc(A2K)
    Wb2 = [A2.take([16, 512], BF16) for _ in range(2)]
    QKT = A2.take([NT, 2, 128], BF16)
    Vst = A2.take([NT, 128], BF16)
    Gst = A2.take([NT, 128], BF16)
    dS = A2.take([NT, 256], F32)
    Stb = A2.take([NT, 256], BF16)
    YTh = A2.take([TOK], BF16)
    Sfb = A2.take([2, 128], F32)
    cs2 = [A2.take([256], F32) for _ in range(2)]
    gnw = [A2.take([2, 128], F32) for _ in range(2)]
    qk_r = [A2.take([2, 128], BF16) for _ in range(2)]
    V22 = [A2.take([256], BF16) for _ in range(2)]
    AT = [A2.take([128], BF16) for _ in range(2)]
    tf_ = [A2.take([128], F32) for _ in range(2)]
    o_ = [A2.take([128], F32) for _ in range(2)]
    yb = [A2.take([128], BF16) for _ in range(2)]
    t2a = [A2.take([2, 128], F32) for _ in range(2)]
    t2b = [A2.take([2, 128], F32) for _ in range(2)]
    bst2 = [A2.take([8], F32) for _ in range(2)]
    def make_head(h):
        wb = Wb2[h % 2]
        wk = ("Wb2", h % 2)
        gs = h % 2
        wkeys = [(wk, j) for j in range(4)]

        def setup():
            for j in range(4):
                load_w(wb[:, :, j * 128:(j + 1) * 128], [(wk, j)],
                       w_in_v[:, :, j * 1024 + h * 128:j * 1024 + h * 128 + 128])
            S.dma("sp", out=gnw[gs][:, 0, :], in_=gnwb_d[:, h * 128:(h + 1) * 128], writes=[("gnw", gs)])
            S.dma("sp", out=gnw[gs][:, 1, :], in_=gnwb_d[:, 1024 + h * 128:1024 + (h + 1) * 128],
                  writes=[("gnw", gs)])

        def l1_a(t, h=h, wb=wb, wkeys=wkeys, gs=gs):
            sl = t % 2
            S.dma("sp", out=cs2[sl], in_=cs_own[t * 128:(t + 1) * 128, :], writes=[("cs2", sl)])
            pa, pk = next_mm()
            for dc in range(DC):
                S.op("pe", lambda e, dc=dc: e.matmul(
                    pa, lhsT=hTo[:, dc, t * 128:(t + 1) * 128], rhs=wb[:, dc, :], start=(dc == 0),
                    stop=(dc == DC - 1)), reads=[("hTo", t)] + wkeys, writes=pk)
            S.op("act", lambda e: e.activation(out=V22[sl][:, 0:128], in_=pa[:, 256:384], func=AF.Copy,
                                               scale=WKF[:, h:h + 1]), reads=pk + TABK, writes=[("V22", sl)])
            S.op("act", lambda e: e.activation(out=V22[sl][:, 128:256], in_=pa[:, 256:384], func=AF.Copy,
                                               scale=WKB[:, h:h + 1]), reads=pk + TABK, writes=[("V22", sl)])
            S.op("act", lambda e: e.activation(out=Vst[:, t, :], in_=pa[:, 256:384], func=AF.Copy),
                 reads=pk, writes=[("Vst", t)])
            S.op("act", lambda e: e.activation(out=Gst[:, t, :], in_=pa[:, 384:512], func=AF.Silu),
                 reads=pk, writes=[("Gst", t)])
            pv = pa[:, 0:256].rearrange("p (h c) -> p h c", h=2, c=128)
            rope((qk_r[sl], ("qk_r", sl)), pv, pk, 2, cs2[sl][:, 0:128], cs2[sl][:, 128:256], ("cs2", sl),
                 t2a[sl], t2b[sl], ("t2", sl))

        def l1_b(t, h=h, wb=wb, wkeys=wkeys, gs=gs):
            sl = t % 2
            pt, ptk = next_tr()
            ptv = pt.rearrange("p (a b) -> p a b", a=4, b=128)
            for j in range(2):
                S.op("pe", lambda e, j=j: e.transpose(out=ptv[:, j, :], in_=qk_r[sl][:, j, :], identity=ident),
                     reads=[("qk_r", sl), "ident"], writes=ptk)
            S.op("act", lambda e: e.activation(out=QKT[:, t, :, :], in_=ptv[:, 0:2, :], func=AF.Copy),
                 reads=ptk, writes=[("QKT", t)])
            pc, pck = next_mm()
            S.op("pe", lambda e: e.matmul(pc[:, 0:256], lhsT=qk_r[sl][:, 1, :], rhs=V22[sl], start=True, stop=True),
                 reads=[("qk_r", sl), ("V22", sl)], writes=pck)
            S.op("act", lambda e: e.activation(out=dS[:, t, :], in_=pc[:, 0:256], func=AF.Copy),
                 reads=pck, writes=[("dS", t)])

        def l1_step(k):
            if k < NT:
                l1_a(k)
            if k >= 1:
                l1_b(k - 1)

        def scan():
            scan_body()

        def scan_body():
            pass
        def scan_real():
            S.op("dve", lambda e, h=h: e.tensor_copy(out=Sfb[:, 0, :], in_=Sacc[:, h, 0:128]), reads=["Sacc"],
                 writes=["Sf"])
            S.op("dve", lambda e, h=h: e.tensor_copy(out=Sfb[:, 1, :], in_=Sacc[:, h, 128:256]), reads=["Sacc"],
                 writes=["Sb"])
            for t in range(NT):
                S.op("act", lambda e, t=t: e.activation(out=Stb[:, t, 0:128], in_=Sfb[:, 0, :], func=AF.Copy),
                     reads=["Sf"], writes=[("Stb", t, 0)])
                if t < NT - 1:
                    S.op("dve", lambda e, t=t, h=h: e.scalar_tensor_tensor(
                        out=Sfb[:, 0, :], in0=Sfb[:, 0, :], scalar=dcy[:, h:h + 1], in1=dS[:, t, 0:128],
                        op0=ALU.mult, op1=ALU.add), reads=["Sf", ("dS", t), "dcy"], writes=["Sf"])
                tb = NT - 1 - t
                S.op("act", lambda e, tb=tb: e.activation(out=Stb[:, tb, 128:256], in_=Sfb[:, 1, :], func=AF.Copy),
                     reads=["Sb"], writes=[("Stb", tb, 1)])
                if tb > 0:
                    S.op("dve", lambda e, tb=tb, h=h: e.scalar_tensor_tensor(
                        out=Sfb[:, 1, :], in0=Sfb[:, 1, :], scalar=dcy[:, 8 + h:9 + h], in1=dS[:, tb, 128:256],
                        op0=ALU.mult, op1=ALU.add), reads=["Sb", ("dS", tb), "dcy"], writes=["Sb"])


        def l2_a(t, h=h, wb=wb, wkeys=wkeys, gs=gs):
            sl = t % 2
            pa, pk = next_mm()
            l2st[t] = (pa, pk)
            S.op("pe", lambda e: e.matmul(pa[:, 0:128], lhsT=QKT[:, t, 1, :], rhs=QKT[:, t, 0, :],
                                          start=True, stop=True), reads=[("QKT", t)], writes=pk)
            S.op("pe", lambda e: e.matmul(pa[:, 256:512], lhsT=QKT[:, t, 0, :], rhs=Stb[:, t, :],
                                          start=True, stop=True),
                 reads=[("QKT", t), ("Stb", t, 0), ("Stb", t, 1)], writes=pk)
            S.op("dve", lambda e: e.tensor_tensor(out=AT[sl], in0=pa[:, 0:128], in1=masks[:, h, :], op=ALU.mult),
                 reads=pk + [("mask", h)], writes=[("AT", sl)])
            S.op("act", lambda e: e.activation(out=tf_[sl], in_=pa[:, 256:384], func=AF.Copy, scale=DQF[:, h:h + 1]),
                 reads=pk + TABK, writes=[("tf", sl)])
            S.op("dve", lambda e: e.scalar_tensor_tensor(
                out=o_[sl], in0=pa[:, 384:512], scalar=DQB[:, h:h + 1], in1=tf_[sl], op0=ALU.mult, op1=ALU.add),
                reads=pk + TABK + [("tf", sl)], writes=[("o", sl)])

        def l2_b(t, h=h, wb=wb, wkeys=wkeys, gs=gs):
            sl = t % 2
            bs = bst2[sl]
            bk = ("bst", sl)
            pb, pbk = ps_full(6 + (t % 2))
            S.op("pe", lambda e: e.matmul(pb[:, 0:128], lhsT=AT[sl], rhs=Vst[:, t, :], start=True, stop=True),
                 reads=[("AT", sl), ("Vst", t)], writes=pbk)
            S.op("dve", lambda e: e.tensor_tensor(out=o_[sl], in0=pb[:, 0:128], in1=o_[sl], op=ALU.add),
                 reads=pbk + [("o", sl)], writes=[("o", sl)])
            S.op("dve", lambda e: e.bn_stats(out=bs[:, 0:6], in_=o_[sl]), reads=[("o", sl)], writes=[bk])
            S.op("dve", lambda e: e.bn_aggr(out=bs[:, 6:8], in_=bs[:, 0:6]), reads=[bk], writes=[bk])
            S.op("act", lambda e: e.activation(out=bs[:, 7:8], in_=bs[:, 7:8], func=AF.Sqrt, bias=epsc),
                 reads=[bk] + CST, writes=[bk])
            S.op("dve", lambda e: e.reciprocal(out=bs[:, 7:8], in_=bs[:, 7:8]), reads=[bk], writes=[bk])
            S.op("dve", lambda e: e.tensor_scalar(out=o_[sl], in0=o_[sl], scalar1=bs[:, 6:7], scalar2=bs[:, 7:8],
                                                  op0=ALU.subtract, op1=ALU.mult),
                 reads=[bk, ("o", sl)], writes=[("o", sl)])
            S.op("dve", lambda e: e.tensor_tensor(out=o_[sl], in0=o_[sl], in1=gnw[gs][:, 0, :], op=ALU.mult),
                 reads=[("o", sl), ("gnw", gs)], writes=[("o", sl)])
            S.op("dve", lambda e: e.tensor_tensor(out=o_[sl], in0=o_[sl], in1=gnw[gs][:, 1, :], op=ALU.add),
                 reads=[("o", sl), ("gnw", gs)], writes=[("o", sl)])
            S.op("dve", lambda e: e.tensor_tensor(out=yb[sl], in0=o_[sl], in1=Gst[:, t, :], op=ALU.mult),
                 reads=[("o", sl), ("Gst", t)], writes=[("yb", sl)])

        def l2_c(t, h=h, wb=wb, wkeys=wkeys, gs=gs):
            sl = t % 2
            pt, ptk = next_tr()
            S.op("pe", lambda e: e.transpose(out=pt[:, 0:128], in_=yb[sl], identity=ident),
                 reads=[("yb", sl), "ident"], writes=ptk)
            S.op("act", lambda e: e.activation(out=YTh[:, t * 128:(t + 1) * 128], in_=pt[:, 0:128], func=AF.Copy),
                 reads=ptk, writes=["YTh"])

        l2st = {}

        def l2_step(k):
            if k < NT:
                l2_a(k)
            if 0 <= k - 1 < NT:
                l2_b(k - 1)
            if 0 <= k - 2 < NT:
                l2_c(k - 2)

        def finish():
            S.dma("sp", out=ytr[:, h, :], in_=YTh, reads=["YTh"], writes=[("ytr", h)])
            if debug is not None and debug[0] == "yth%d" % h:
                tap("yth%d" % h, YTh, ["YTh"])

        return dict(setup=setup, l1_step=l1_step, scan=scan_real, l2_step=l2_step, finish=finish)

    HD = [make_head(h) for h in range(NRH)]
    HD[0]["setup"]()
    for k in range(NT + 1):
        HD[0]["l1_step"](k)
    HD[0]["scan"]()
    for h in range(NRH):
        nxt = HD[h + 1] if h + 1 < NRH else None
        if nxt is not None:
            nxt["setup"]()
        for k in range(NT + 4):
            if k < NT + 2:
                HD[h]["l2_step"](k)
            if nxt is not None and 0 <= k - 3 <= NT:
                nxt["l1_step"](k - 3)
        HD[h]["finish"]()
        if nxt is not None:
            nxt["scan"]()
    S.barrier()


    if stop == 2:
        S.finish(); S.emit(nc, stack); stack.close(); return nc
    A3 = Alloc(PBASE)
    KT = A3.take([2, NK * 128], BF16)
    Vt = A3.take([NK, 256], BF16)
    A3K = A3.off
    for kvh in range(2):
        S.dma("sp", out=KT[:, kvh, :], in_=ksc[:, kvh, :], writes=[("KT", kvh)])
    NQ = 4 if NK >= 4 else 1
    for q4 in range(NQ):
        a, b = q4 * NK // NQ, (q4 + 1) * NK // NQ
        S.dma("sp", out=Vt[:, a:b, :], in_=vsc[:, a:b, :], writes=[("Vt", q4)])
    tap("kt", KT, [("KT", 0), ("KT", 1)])
    tap("vt", Vt, [("Vt", q4) for q4 in range(NQ)])
    S.barrier()
    KVK = [("KT", kt) for kt in range(NK)] + [("Vt", kt) for kt in range(NK)]


    if stop == 3:
        S.finish(); S.emit(nc, stack); stack.close(); return nc
    A4 = Alloc(A3K)
    xg = A4.take([GT, D], F32)
    hTg = A4.take([16, GW], BF16)
    NPG = 24
    R = A4.take([NPG, GW], BF16)
    Wb4 = [A4.take([16, 512], BF16) for _ in range(2)]
    xn4 = A4.take([D], BF16)
    PT = [A4.take([GW], BF16) for _ in range(3)]
    rec = [A4.take([GW], F32) for _ in range(2)]
    cs4 = [A4.take([256], F32) for _ in range(2)]
    qr4 = [A4.take([4, 128], BF16) for _ in range(2)]
    t4A = A4.take([4, 128], F32)
    t4B = A4.take([4, 128], F32)
    t4U = A4.take([4, 128], F32)
    t4C = A4.take([128], F32)
    t4S = A4.take([128], F32)
    t4T = A4.take([4, 128], F32)
    Pc = [A4.take([256], BF16) for _ in range(2)]
    PnT = [A4.take([2, 128], BF16) for _ in range(2)]
    ocT = [A4.take([4, 128], BF16)] * 2
    rl = [A4.take([GW], F32) for _ in range(2)]
    RK = lambda a, b: [("R", p) for p in range(a, b)]
    wfin_v = R[:, 16:24, :].rearrange("p a b -> p (a b)").bitcast(F32)
    wcnt = [0]

    def next_w():
        k = wcnt[0] % 2
        wcnt[0] += 1
        return Wb4[k], ("Wb4", k)

    for g in range(NG):
        tok0 = g * GW
        for i in range(GT):
            t = g * GT + i
            norm_tile(x_own[t * 128:(t + 1) * 128, :], xg[:, i, :], ("xg", i), xn4, "xn4", SM_NW,
                      lambda gq, i=i: hTg[:, 4 * gq:4 * gq + 4, i * 128:(i + 1) * 128], ("hTg", i))
        def q_tr(blk, i, qs):
            for hp in range(2):
                pt, ptk = next_tr()
                ptv = pt.rearrange("p (a b) -> p a b", a=4, b=128)
                for j in range(2):
                    S.op("pe", lambda e, j=j, hp=hp, ptv=ptv: e.transpose(
                        out=ptv[:, j, :], in_=qr4[qs][:, 2 * hp + j, :], identity=ident),
                        reads=[("qr4", qs), "ident"], writes=ptk)
                h0 = blk * 4 + 2 * hp
                S.op("act", lambda e, ptv=ptv, h0=h0: e.activation(
                    out=R[:, h0:h0 + 2, i * 128:(i + 1) * 128], in_=ptv[:, 0:2, :], func=AF.Copy),
                    reads=ptk, writes=RK(h0, h0 + 2))

        qpend = None
        for blk in range(2):
            wb, wk = next_w()
            load_w(wb, [wk], w_in_v[:, :, 4096 + blk * 512:4096 + (blk + 1) * 512])
            for i in range(GT):
                t = g * GT + i
                sl = i % 2
                S.dma("sp", out=cs4[sl], in_=cs_own[t * 128:(t + 1) * 128, :], writes=[("cs4", sl)])
                pa, pk = next_mm()
                for dc in range(DC):
                    S.op("pe", lambda e, dc=dc, pa=pa, i=i, wb=wb: e.matmul(
                        pa, lhsT=hTg[:, dc, i * 128:(i + 1) * 128], rhs=wb[:, dc, :], start=(dc == 0),
                        stop=(dc == DC - 1)), reads=[("hTg", i), wk], writes=pk)
                pv = pa.rearrange("p (h c) -> p h c", h=4, c=128)
                qs = (blk * GT + i) % 2
                qknorm_rope((qr4[qs], ("qr4", qs)), pv, pk, 4, SM_GQ, cs4[sl], ("cs4", sl), t4A, t4B, t4U, t4C,
                            t4S, t4T, "t4")
                if qpend is not None:
                    q_tr(*qpend)
                qpend = (blk, i, qs)
        q_tr(*qpend)
        qpend = None
        its = [(h, kt) for h in range(8) for kt in range(NK)]
        SK = 2
        for idx in range(len(its) + SK):
            if idx < len(its):
                h, kt = its[idx]
                kvh = h // 4
                pa, pk = next_mm()
                ps_ = idx % 3
                S.op("pe", lambda e, pa=pa, kt=kt, kvh=kvh, h=h: e.matmul(
                    pa, lhsT=KT[:, kvh, kt * 128:(kt + 1) * 128], rhs=R[:, h, :], start=True, stop=True),
                    reads=[("R", h)], writes=pk)
                S.op("act", lambda e, pa=pa, ps_=ps_: e.activation(out=PT[ps_], in_=pa, func=AF.Exp, scale=ISQ,
                                                                  bias=negc),
                     reads=pk + ["negc"], writes=[("PT", ps_)])
            j = idx - SK
            if j >= 0:
                h, kt = its[j]
                kvh = h // 4
                ps_ = j % 3
                bo, bz = (6, 7) if h % 2 == 0 else (0, 1)
                po, pok = ps_full(bo)
                pz, pzk = ps_full(bz)
                S.op("pe", lambda e, po=po, kt=kt, kvh=kvh, ps_=ps_: e.matmul(
                    po, lhsT=Vt[:, kt, kvh * 128:(kvh + 1) * 128], rhs=PT[ps_], start=(kt == 0),
                    stop=(kt == NK - 1)), reads=[("PT", ps_)], writes=pok)
                S.op("pe", lambda e, pz=pz, kt=kt, ps_=ps_: e.matmul(
                    pz, lhsT=ones, rhs=PT[ps_], start=(kt == 0), stop=(kt == NK - 1)),
                    reads=[("PT", ps_), "ones"], writes=pzk)
                if kt == NK - 1:
                    rc = rec[h % 2]
                    S.op("dve", lambda e, pz=pz, rc=rc: e.reciprocal(out=rc, in_=pz), reads=pzk,
                         writes=[("rec", h % 2)])
                    S.op("dve", lambda e, po=po, h=h, rc=rc: e.tensor_tensor(out=R[:, 16 + h, :], in0=po, in1=rc,
                                                                             op=ALU.mult),
                         reads=pok + [("rec", h % 2)], writes=[("R", 16 + h)])
        S.dma("sp", out=R[:, 8:16, :], in_=ytr[:, :, tok0:tok0 + GW], reads=[("ytr", h) for h in range(8)],
              writes=RK(8, 16))
        if g == 0:
            tap("yt", R[:, 8:24, :], RK(8, 24))
            tap("qt", R[:, 0:8, :], RK(0, 8))
        for c in range(4):
            wb, wk = next_w()
            load_w(wb, [wk], w_out_v[:, :, c * 512:(c + 1) * 512])
            for i in range(GT):
                pa, pk = next_mm()
                for fc in range(16):
                    S.op("pe", lambda e, fc=fc, pa=pa, i=i, wb=wb: e.matmul(
                        pa, lhsT=R[:, 8 + fc, i * 128:(i + 1) * 128], rhs=wb[:, fc, :], start=(fc == 0),
                        stop=(fc == 15)), reads=[("R", 8 + fc), wk], writes=pk)
                xv = xg[:, i, c * 512:(c + 1) * 512]
                S.op("dve", lambda e, pa=pa, xv=xv: e.tensor_tensor(out=xv, in0=pa, in1=xv, op=ALU.add),
                     reads=pk + [("xg", i)], writes=[("xg", i)])
        if g == 0:
            tap("xg1", xg, [("xg", i) for i in range(GT)])
        for i in range(GT):
            norm_tile(None, xg[:, i, :], ("xg", i), xn4, "xn4", SM_NW + 16,
                      lambda gq, i=i: hTg[:, 4 * gq:4 * gq + 4, i * 128:(i + 1) * 128], ("hTg", i), load=False)
        wb, wk = next_w()
        load_w(wb, [wk], w_cq_v)
        for hd in range(4):
            pa, pk = next_mm()
            for dc in range(DC):
                S.op("pe", lambda e, dc=dc, pa=pa, hd=hd, wb=wb: e.matmul(
                    pa[:, 0:GW], lhsT=wb[:, dc, hd * 128:(hd + 1) * 128], rhs=hTg[:, dc, :], start=(dc == 0),
                    stop=(dc == DC - 1)), reads=[("hTg", i) for i in range(GT)] + [wk], writes=pk)
            S.op("act", lambda e, pa=pa, hd=hd: e.activation(out=R[:, hd, :], in_=pa[:, 0:GW], func=AF.Copy),
                 reads=pk, writes=[("R", hd)])
        wb, wk = next_w()
        wco = wb.rearrange("p a b -> p (a b)").rearrange("p (a b) -> p a b", a=4, b=2048)
        load_w(wco, [wk], w_co_v)
        cits = [(i, hd) for i in range(GT) for hd in range(4)]
        cst_ = {}

        def ca_A(k):
            i, hd = cits[k]
            pa, pk = next_mm()
            s4, sk = next_sc()
            sl = k % 2
            S.op("pe", lambda e: e.matmul(pa[:, 0:256], lhsT=R[:, hd, i * 128:(i + 1) * 128], rhs=KcT[:, hd, :],
                                          start=True, stop=True), reads=[("R", hd), "KcT"], writes=pk)
            S.op("dve", lambda e: e.tensor_reduce(out=s4[:, 0:1], in_=pa[:, 0:256], axis=AX.X, op=ALU.max),
                 reads=pk, writes=[sk])
            S.op("dve", lambda e: e.tensor_scalar(out=s4[:, 0:1], in0=s4[:, 0:1], scalar1=-ISQ, scalar2=None,
                                                  op0=ALU.mult), reads=[sk], writes=[sk])
            S.op("act", lambda e: e.activation(out=Pc[sl], in_=pa[:, 0:256], func=AF.Exp, scale=ISQ,
                                               bias=s4[:, 0:1], accum_out=s4[:, 1:2]),
                 reads=pk + [sk], writes=[("Pc", sl), sk])
            S.op("dve", lambda e: e.reciprocal(out=s4[:, 2:3], in_=s4[:, 1:2]), reads=[sk], writes=[sk])
            S.op("dve", lambda e: e.tensor_scalar(out=Pc[sl], in0=Pc[sl], scalar1=s4[:, 2:3], scalar2=None,
                                                  op0=ALU.mult), reads=[("Pc", sl), sk], writes=[("Pc", sl)])

        def ca_B(k):
            sl = k % 2
            pt, ptk = next_tr()
            ptv = pt.rearrange("p (a b) -> p a b", a=4, b=128)
            for j in range(2):
                S.op("pe", lambda e, j=j: e.transpose(out=ptv[:, j, :], in_=Pc[sl][:, j * 128:(j + 1) * 128],
                                                      identity=ident), reads=[("Pc", sl), "ident"], writes=ptk)
            S.op("act", lambda e: e.activation(out=PnT[sl], in_=ptv[:, 0:2, :], func=AF.Copy),
                 reads=ptk, writes=[("PnT", sl)])

        def ca_C(k, wco=wco, wk=wk):
            i, hd = cits[k]
            sl = k % 2
            osl = i % 2
            pa, pk = next_mm()
            for j in range(2):
                S.op("pe", lambda e, j=j: e.matmul(pa[:, 0:128], lhsT=Vc[:, j, hd * 128:(hd + 1) * 128],
                                                   rhs=PnT[sl][:, j, :], start=(j == 0), stop=(j == 1)),
                     reads=[("PnT", sl), "Vc"], writes=pk)
            S.op("act", lambda e: e.activation(out=ocT[osl][:, hd, :], in_=pa[:, 0:128], func=AF.Copy),
                 reads=pk, writes=[("ocT", 0, hd)])
            if hd == 3:
                for c in range(4):
                    pb, pbk = next_mm()
                    for h2 in range(4):
                        S.op("pe", lambda e, h2=h2, c=c, pb=pb: e.matmul(
                            pb, lhsT=ocT[osl][:, h2, :], rhs=wco[:, h2, c * 512:(c + 1) * 512], start=(h2 == 0),
                            stop=(h2 == 3)), reads=[("ocT", 0, h2), wk], writes=pbk)
                    xv = xg[:, i, c * 512:(c + 1) * 512]
                    S.op("dve", lambda e, pb=pb, xv=xv: e.tensor_tensor(out=xv, in0=pb, in1=xv, op=ALU.add),
                         reads=pbk + [("xg", i)], writes=[("xg", i)])

        for k in range(len(cits) + 2):
            if k < len(cits):
                ca_A(k)
            if 0 <= k - 1 < len(cits):
                ca_B(k - 1)
            if 0 <= k - 2 < len(cits):
                ca_C(k - 2)
        if g == 0:
            tap("xg2", xg, [("xg", i) for i in range(GT)])
        for i in range(GT):
            norm_tile(None, xg[:, i, :], ("xg", i), xn4, "xn4", SM_NW + 48,
                      lambda gq, i=i: hTg[:, 4 * gq:4 * gq + 4, i * 128:(i + 1) * 128], ("hTg", i), load=False)
        HK = [("hTg", i) for i in range(GT)]
        ucnt = 0
        for qf in range(4):
            for ub in range(4):
                wb, wk = next_w()
                c0 = qf * 2048 + ub * 512
                load_w(wb, [wk], w_up_v[:, :, c0:c0 + 512])
                for j in range(4):
                    f = ub * 4 + j
                    b = (0, 1, 6, 7)[ucnt % 4]
                    pa, pk = ps_full(b)
                    for dc in range(DC):
                        S.op("pe", lambda e, dc=dc, pa=pa, j=j, wb=wb: e.matmul(
                            pa[:, 0:GW], lhsT=wb[:, dc, j * 128:(j + 1) * 128], rhs=hTg[:, dc, :], start=(dc == 0),
                            stop=(dc == DC - 1)), reads=HK + [wk], writes=pk)
                    rs_ = ucnt % 2
                    ucnt += 1
                    S.op("act", lambda e, pa=pa, rs_=rs_: e.activation(out=rl[rs_], in_=pa[:, 0:GW], func=AF.Relu),
                         reads=pk, writes=[("rl", rs_)])
                    S.op("dve", lambda e, rs_=rs_, f=f: e.tensor_tensor(out=R[:, f, :], in0=rl[rs_], in1=rl[rs_],
                                                                       op=ALU.mult),
                         reads=[("rl", rs_)], writes=[("R", f)])
            for c in range(4):
                wb, wk = next_w()
                load_w(wb, [wk], w_dn_v[:, qf * 16:(qf + 1) * 16, c * 512:(c + 1) * 512])
                for i in range(GT):
                    pa, pk = ps_full(2 + i)
                    for f in range(16):
                        S.op("pe", lambda e, f=f, pa=pa, i=i, wb=wb: e.matmul(
                            pa, lhsT=R[:, f, i * 128:(i + 1) * 128], rhs=wb[:, f, :], start=(f == 0),
                            stop=(f == 15)), reads=[("R", f), wk], writes=pk)
                    xv = xg[:, i, c * 512:(c + 1) * 512]
                    S.op("dve", lambda e, pa=pa, xv=xv: e.tensor_tensor(out=xv, in0=pa, in1=xv, op=ALU.add),
                         reads=pk + [("xg", i)], writes=[("xg", i)])
        if g == 0:
            tap("xg3", xg, [("xg", i) for i in range(GT)])
        S.dma("sp", out=wfin_v, in_=wfin_d, writes=RK(16, 24))
        for i in range(GT):
            t = g * GT + i
            s4, sk = next_sc()
            S.op("act", lambda e, i=i, s4=s4: e.activation(out=xn4, in_=xg[:, i, :], func=AF.Square,
                                                          accum_out=s4[:, 0:1]),
                 reads=[("xg", i)], writes=["xn4", sk])
            S.op("act", lambda e, s4=s4: e.activation(out=s4[:, 1:2], in_=s4[:, 0:1], func=AF.Sqrt, scale=1.0 / D,
                                                     bias=epsc), reads=[sk] + CST, writes=[sk])
            S.op("dve", lambda e, s4=s4: e.reciprocal(out=s4[:, 2:3], in_=s4[:, 1:2]), reads=[sk], writes=[sk])
            S.op("dve", lambda e, i=i, s4=s4: e.scalar_tensor_tensor(
                out=xg[:, i, :], in0=xg[:, i, :], scalar=s4[:, 2:3], in1=wfin_v, op0=ALU.mult, op1=ALU.mult),
                reads=[("xg", i), sk] + RK(16, 24), writes=[("xg", i)])
            S.dma("sp", out=out_d[t * 128:(t + 1) * 128, :], in_=xg[:, i, :], reads=[("xg", i)],
                  writes=[("out", t)])
    S.finish()
    S.emit(nc, stack)
    stack.close()
    return nc


def _rope_cs(seq):
    rows = seq // 64
    row = np.repeat(np.arange(rows, dtype=np.float32), 64)
    col = np.tile(np.arange(64, dtype=np.float32), rows)
    inv = (1.0 / (np.float32(10000.0) ** (np.arange(0, 64, 2, dtype=np.float32) / np.float32(64)))).astype(np.float32)
    ar = (row[:, None] * inv[None, :]).astype(np.float32)
    ac = (col[:, None] * inv[None, :]).astype(np.float32)
    cr, sr, cc, sc_ = np.cos(ar), np.sin(ar), np.cos(ac), np.sin(ac)
    C = np.concatenate([cr, cr, cc, cc], axis=1)
    Sg = np.concatenate([-sr, sr, -sc_, sc_], axis=1)
    return np.concatenate([C, Sg], axis=1).astype(np.float32)


def _swap32(g):
    v = g.reshape(2, 2, 32)
    return v[:, ::-1, :].reshape(128)


_NC_CACHE = {}


def kernel(x, mem, norm_mix_w, w_in, ret_decay_fwd, ret_decay_bwd, ret_gn_w, ret_gn_b,
           attn_q_norm_w, attn_k_norm_w, w_out, norm_cross_w, norm_mem_w,
           w_cross_q, w_cross_k, w_cross_v, w_cross_o, norm_mlp_w,
           w_mlp_up, w_mlp_down, norm_final_w, _debug=None):
    f = lambda a: np.ascontiguousarray(np.asarray(a, dtype=np.float32))
    x = f(x)
    mem = f(mem)
    B, SEQ, _ = x.shape
    TOK = SEQ // 4
    NT = TOK // 128
    NO = 3 * NT
    key = (SEQ, None if _debug is None else _debug[0])
    if key not in _NC_CACHE:
        _NC_CACHE[key] = build(SEQ, _debug)
    nc = _NC_CACHE[key]
    cs = _rope_cs(SEQ)
    idx = np.arange(128, dtype=np.float32)
    relf = np.where(idx[None, :] >= idx[:, None], idx[None, :] - idx[:, None], BIG).astype(np.float32)
    relb = np.where(idx[:, None] > idx[None, :], idx[:, None] - idx[None, :], BIG).astype(np.float32)
    nw = np.concatenate([f(w).reshape(-1)[:D].reshape(16, 128).T for w in
                         (norm_mix_w, norm_cross_w, norm_mem_w, norm_mlp_w)], axis=1)
    rep = lambda v: np.broadcast_to(f(v).reshape(1, -1), (128, f(v).size))
    gq = f(attn_q_norm_w).reshape(128)
    gk = f(attn_k_norm_w).reshape(128)
    common = {
        "ident": np.eye(128, dtype=np.float32),
        "gnwb": np.ascontiguousarray(np.concatenate([rep(ret_gn_w), rep(ret_gn_b)], axis=1)),
        "wfin": np.ascontiguousarray(rep(norm_final_w)),
        "w_in": f(w_in)[0], "w_out": f(w_out)[0], "w_cq": f(w_cross_q)[0], "w_ck": f(w_cross_k)[0],
        "w_cv": f(w_cross_v)[0], "w_co": f(w_cross_o)[0], "w_up": f(w_mlp_up)[0], "w_dn": f(w_mlp_down)[0],
    }
    in_maps = []
    for c in range(8):
        b, j = c // 4, c % 4
        own = slice(j * TOK, (j + 1) * TOK)
        oth_idx = np.concatenate([np.arange(s * TOK, (s + 1) * TOK) for s in range(4) if s != j])
        ef = np.where(oth_idx < j * TOK, j * TOK - 1 - oth_idx, BIG).astype(np.float32)
        eb = np.where(oth_idx >= (j + 1) * TOK, oth_idx - (j + 1) * TOK, BIG).astype(np.float32)
        eo = np.stack([ef.reshape(NO, 128).T, eb.reshape(NO, 128).T], axis=2).reshape(128, 2 * NO)
        sm = np.concatenate([
            nw, rep(ret_decay_fwd), rep(ret_decay_bwd),
            np.stack([idx + 1, 128 - idx, 127 - idx, idx], axis=1),
            rep(gq), rep(_swap32(gq)), rep(gk), rep(_swap32(gk)),
            relf, relb, eo], axis=1).astype(np.float32)
        m = dict(common)
        m.update({
            "x_own": np.ascontiguousarray(x[b, own]),
            "x_oth": np.ascontiguousarray(x[b, oth_idx]),
            "cs_own": np.ascontiguousarray(cs[own]),
            "cs_oth": np.ascontiguousarray(cs[oth_idx]),
            "smalls": np.ascontiguousarray(sm),
            "mem": np.ascontiguousarray(mem[b]),
        })
        in_maps.append(m)
    res = run_bass_kernel_spmd(nc, in_maps, core_ids=list(range(8)))
    out = np.empty((B, SEQ, D), dtype=np.float32)
    for c in range(8):
        b, j = c // 4, c % 4
        out[b, j * TOK:(j + 1) * TOK] = res.results[c]["out"]
    if _debug is not None:
        return out, [res.results[c]["dbg"] for c in range(8)]
    return out
```
